# Optimizing a Trainium2 kernel written in Bass

```python
import jax, jax.numpy as jnp
from jax import lax
import numpy as np

D_MODEL = 1024
BATCH = 8
SEQ = 2048
DEPTH = 2

CHUNK = 64
N_MIXERS = 2
N_ATTN_LAYERS = (DEPTH + 1) // 2
N_LRU_LAYERS = DEPTH // 2
N_HEADS = 16
HEAD_DIM = D_MODEL // N_HEADS
LEFT_CHUNKS = 8
BAND_CHUNKS = LEFT_CHUNKS + 1
MAX_REL_DIST = 128
LRU_WIDTH = D_MODEL
LRU_BLOCKS = 16
LRU_BLOCK_W = LRU_WIDTH // LRU_BLOCKS
LRU_C = 8.0
CONV_WIDTH = 4
D_FF = 2816
NORM_EPS = 1e-6
MASK_VALUE = -1e30

kernel_name = "hybrid_chunked_attn_rglru_macaron"


def rms_norm(x, gain):
    xf = x.astype(jnp.float32)
    y = xf * lax.rsqrt(jnp.mean(xf * xf, axis=-1, keepdims=True) + NORM_EPS)
    return (y * gain.astype(jnp.float32)).astype(x.dtype)


def swiglu_ffn(h, w_in, w_out):
    gate, up = jnp.split(h @ w_in, 2, axis=-1)
    return (jax.nn.silu(gate) * up) @ w_out


def chunk_band(t):
    nc = t.shape[1]
    tp = jnp.pad(t, ((0, 0), (LEFT_CHUNKS, 0), (0, 0), (0, 0), (0, 0)))
    return jnp.concatenate([tp[:, j:j + nc] for j in range(BAND_CHUNKS)], axis=2)


def chunked_relpos_attention(h, w_in, q_gain, k_gain, rel_bias, w_out):
    b, s, d = h.shape
    nc = s // CHUNK
    q, k, v = jnp.split(h @ w_in, 3, axis=-1)
    q = rms_norm(q.reshape(b, nc, CHUNK, N_HEADS, HEAD_DIM), q_gain)
    k = rms_norm(k.reshape(b, nc, CHUNK, N_HEADS, HEAD_DIM), k_gain)
    v = v.reshape(b, nc, CHUNK, N_HEADS, HEAD_DIM)
    k_band = chunk_band(k)
    v_band = chunk_band(v)
    scores = jnp.einsum('bnqhd,bnkhd->bnhqk', q.astype(jnp.float32),
                        k_band.astype(jnp.float32)) * (HEAD_DIM ** -0.5)
    q_pos = jnp.arange(CHUNK)[:, None] + LEFT_CHUNKS * CHUNK
    k_pos = jnp.arange(BAND_CHUNKS * CHUNK)[None, :]
    rel = jnp.clip(q_pos - k_pos, -MAX_REL_DIST, MAX_REL_DIST) + MAX_REL_DIST
    bias = rel_bias[:, rel].astype(jnp.float32)
    key_chunk = jnp.arange(nc)[:, None] - LEFT_CHUNKS + k_pos // CHUNK
    valid = key_chunk >= 0
    scores = jnp.where(valid[None, :, None, None, :], scores + bias[None, None], MASK_VALUE)
    probs = jax.nn.softmax(scores, axis=-1).astype(v.dtype)
    out = jnp.einsum('bnhqk,bnkhd->bnqhd', probs, v_band).reshape(b, s, d)
    return out @ w_out


def causal_depthwise_conv(x, w, bias):
    s = x.shape[1]
    xp = jnp.pad(x, ((0, 0), (CONV_WIDTH - 1, 0), (0, 0)))
    out = xp[:, 0:s] * w[0]
    for j in range(1, CONV_WIDTH):
        out = out + xp[:, j:j + s] * w[j]
    return out + bias


def block_diag_linear(x, w, bias):
    b, s, _ = x.shape
    xh = x.reshape(b, s, LRU_BLOCKS, LRU_BLOCK_W)
    return jnp.einsum('bshi,hij->bshj', xh, w).reshape(b, s, LRU_WIDTH) + bias


def _lru_combine(left, right):
    a_l, b_l = left
    a_r, b_r = right
    return a_l * a_r, a_r * b_l + b_r


def rglru_block(h, w_in, conv_w, conv_b, w_a, b_a, w_x, b_x, lam, w_out):
    xb, gate = jnp.split(h @ w_in, 2, axis=-1)
    xb = causal_depthwise_conv(xb, conv_w, conv_b)
    r = jax.nn.sigmoid(block_diag_linear(xb, w_a, b_a).astype(jnp.float32))
    i = jax.nn.sigmoid(block_diag_linear(xb, w_x, b_x).astype(jnp.float32))
    log_a = -LRU_C * r * jax.nn.softplus(-lam.astype(jnp.float32))
    a = jnp.exp(log_a)
    mult = jnp.sqrt(-jnp.expm1(2.0 * log_a))
    u = mult * (i * xb.astype(jnp.float32))
    _, hseq = lax.associative_scan(_lru_combine, (a, u), axis=1)
    y = hseq.astype(h.dtype) * jax.nn.gelu(gate)
    return y @ w_out


def _normal(k, shape, scale):
    return jax.random.normal(k, shape, jnp.float32) * scale


def setup_inputs(seed: int = 0) -> dict:
    key = jax.random.key(seed)
    ks = jax.random.split(key, 24)
    na, nl = N_ATTN_LAYERS, N_LRU_LAYERS
    u = jax.random.uniform(ks[20], (nl, LRU_WIDTH), jnp.float32, minval=0.9, maxval=0.999)
    s_lam = u ** (1.0 / LRU_C)
    lam = jnp.log(s_lam) - jnp.log1p(-s_lam)
    return {
        "x": _normal(ks[0], (BATCH, SEQ, D_MODEL), 1.0),
        "norm_ffn_pre": 1.0 + _normal(ks[1], (DEPTH, D_MODEL), 0.1),
        "ffn_pre_w_in": _normal(ks[2], (DEPTH, D_MODEL, 2 * D_FF), D_MODEL ** -0.5),
        "ffn_pre_w_out": _normal(ks[3], (DEPTH, D_FF, D_MODEL), D_FF ** -0.5),
        "norm_mix": 1.0 + _normal(ks[4], (DEPTH, D_MODEL), 0.1),
        "norm_ffn_post": 1.0 + _normal(ks[5], (DEPTH, D_MODEL), 0.1),
        "ffn_post_w_in": _normal(ks[6], (DEPTH, D_MODEL, 2 * D_FF), D_MODEL ** -0.5),
        "ffn_post_w_out": _normal(ks[7], (DEPTH, D_FF, D_MODEL), D_FF ** -0.5),
        "attn_w_in": _normal(ks[8], (na, D_MODEL, 3 * D_MODEL), D_MODEL ** -0.5),
        "attn_q_gain": 1.0 + _normal(ks[9], (na, HEAD_DIM), 0.1),
        "attn_k_gain": 1.0 + _normal(ks[10], (na, HEAD_DIM), 0.1),
        "attn_rel_bias": _normal(ks[11], (na, N_HEADS, 2 * MAX_REL_DIST + 1), 0.5),
        "attn_w_out": _normal(ks[12], (na, D_MODEL, D_MODEL), D_MODEL ** -0.5),
        "lru_w_in": _normal(ks[13], (nl, D_MODEL, 2 * LRU_WIDTH), D_MODEL ** -0.5),
        "lru_conv_w": _normal(ks[14], (nl, CONV_WIDTH, LRU_WIDTH), CONV_WIDTH ** -0.5),
        "lru_conv_b": _normal(ks[15], (nl, LRU_WIDTH), 0.02),
        "lru_w_a": _normal(ks[16], (nl, LRU_BLOCKS, LRU_BLOCK_W, LRU_BLOCK_W), LRU_BLOCK_W ** -0.5),
        "lru_b_a": _normal(ks[17], (nl, LRU_WIDTH), 0.1),
        "lru_w_x": _normal(ks[18], (nl, LRU_BLOCKS, LRU_BLOCK_W, LRU_BLOCK_W), LRU_BLOCK_W ** -0.5),
        "lru_b_x": _normal(ks[19], (nl, LRU_WIDTH), 0.1),
        "lru_lambda": lam,
        "lru_w_out": _normal(ks[21], (nl, LRU_WIDTH, D_MODEL), LRU_WIDTH ** -0.5),
    }


def reference(x, norm_ffn_pre, ffn_pre_w_in, ffn_pre_w_out, norm_mix, norm_ffn_post,
              ffn_post_w_in, ffn_post_w_out, attn_w_in, attn_q_gain, attn_k_gain,
              attn_rel_bias, attn_w_out, lru_w_in, lru_conv_w, lru_conv_b, lru_w_a,
              lru_b_a, lru_w_x, lru_b_x, lru_lambda, lru_w_out):
    for layer in range(DEPTH):
        x = x + 0.5 * swiglu_ffn(rms_norm(x, norm_ffn_pre[layer]),
                                 ffn_pre_w_in[layer], ffn_pre_w_out[layer])
        h = rms_norm(x, norm_mix[layer])
        j = layer // N_MIXERS
        if layer % N_MIXERS == 0:
            x = x + chunked_relpos_attention(h, attn_w_in[j], attn_q_gain[j], attn_k_gain[j],
                                             attn_rel_bias[j], attn_w_out[j])
        else:
            x = x + rglru_block(h, lru_w_in[j], lru_conv_w[j], lru_conv_b[j], lru_w_a[j],
                                lru_b_a[j], lru_w_x[j], lru_b_x[j], lru_lambda[j], lru_w_out[j])
        x = x + 0.5 * swiglu_ffn(rms_norm(x, norm_ffn_post[layer]),
                                 ffn_post_w_in[layer], ffn_post_w_out[layer])
    return x
```

```python
import contextlib
import numpy as np
import concourse.bass as bass
import concourse.mybir as mybir
from concourse.bass_utils import run_bass_kernel_spmd

F32 = mybir.dt.float32
BF16 = mybir.dt.bfloat16
AF = mybir.ActivationFunctionType
ALU = mybir.AluOpType

D = 1024
SEQ = 2048
NB = 8
NCH = 8
NT = 4
TT = 512
DFF = 2816
NF = 22
EPS = 1e-6
NHEAD = 16
GROUPS = [5, 5, 4, 4, 4]
GMAX = 5

ENGS = ("pe", "act", "dve", "pool", "sp")


class T:
    __slots__ = ("name", "last_w", "readers")

    def __init__(self, name=""):
        self.name = name
        self.last_w = None
        self.readers = []


class Op:
    __slots__ = ("eng", "fn", "idx", "deps", "needs_inc", "inc_val", "is_dma", "dma_sem_id", "dma_val")

    def __init__(self, eng, fn, is_dma=False):
        self.eng = eng
        self.fn = fn
        self.idx = -1
        self.deps = []
        self.needs_inc = False
        self.inc_val = 0
        self.is_dma = is_dma
        self.dma_sem_id = None
        self.dma_val = 0


class Prog:
    def __init__(self):
        self.ops = {e: [] for e in ENGS}
        self.pending = {e: [] for e in ENGS}
        self.dma_counts = []

    def new_dma_sem(self):
        self.dma_counts.append(0)
        return len(self.dma_counts) - 1

    def _add(self, op, reads, writes):
        deps = []
        for t in reads:
            if t.last_w is not None:
                deps.append(t.last_w)
        for t in writes:
            if t.last_w is not None:
                deps.append(t.last_w)
            deps.extend(t.readers)
        deps.extend(self.pending[op.eng])
        self.pending[op.eng] = []
        seen = set()
        for d in deps:
            if d is op or id(d) in seen:
                continue
            seen.add(id(d))
            op.deps.append(d)
        for t in reads:
            t.readers.append(op)
        for t in writes:
            t.last_w = op
            t.readers = []
        op.idx = len(self.ops[op.eng])
        self.ops[op.eng].append(op)
        return op

    def op(self, eng, fn, reads=(), writes=(), extra=()):
        self.pending[eng].extend(extra)
        return self._add(Op(eng, fn), list(reads), list(writes))

    def dma(self, eng, fn, sem_id, reads=(), writes=(), extra=()):
        op = Op(eng, fn, is_dma=True)
        op.dma_sem_id = sem_id
        self.dma_counts[sem_id] += 16
        op.dma_val = self.dma_counts[sem_id]
        self.pending[eng].extend(extra)
        return self._add(op, list(reads), list(writes))

    def barrier(self, engs=("pe", "act", "dve")):
        lasts = [self.ops[e][-1] for e in engs if self.ops[e]]
        for e in engs:
            self.pending[e].extend(lasts)

    def emit(self, nc, final_wait_ops=()):
        for e in ENGS:
            for op in self.ops[e]:
                for d in op.deps:
                    if not d.is_dma:
                        d.needs_inc = True
        for e in ENGS:
            c = 0
            for op in self.ops[e]:
                if (not op.is_dma) and op.needs_inc:
                    c += 1
                    op.inc_val = c
        nd = len(self.dma_counts)
        with contextlib.ExitStack() as st:
            esem = {e: st.enter_context(nc.semaphore("s_" + e)) for e in ENGS}
            dsem = [st.enter_context(nc.semaphore("d_%d" % i)) for i in range(nd)]
            block = st.enter_context(nc.Block())

            def run(e, eng):
                waited_e = {x: 0 for x in ENGS}
                waited_d = [0] * nd
                for op in self.ops[e]:
                    for d in op.deps:
                        if d.is_dma:
                            if waited_d[d.dma_sem_id] < d.dma_val:
                                eng.wait_ge(dsem[d.dma_sem_id], d.dma_val)
                                waited_d[d.dma_sem_id] = d.dma_val
                        else:
                            if d.eng == e and e == "pe":
                                continue
                            if waited_e[d.eng] < d.inc_val:
                                eng.wait_ge(esem[d.eng], d.inc_val)
                                waited_e[d.eng] = d.inc_val
                    ins = op.fn(eng)
                    if op.is_dma:
                        ins.then_inc(dsem[op.dma_sem_id], 16)
                    elif op.needs_inc:
                        ins.then_inc(esem[e], 1)
                if e == "sp":
                    for op in final_wait_ops:
                        if waited_d[op.dma_sem_id] < op.dma_val:
                            eng.wait_ge(dsem[op.dma_sem_id], op.dma_val)
                            waited_d[op.dma_sem_id] = op.dma_val

            @block.sync
            def _(eng):
                run("sp", eng)

            @block.tensor
            def _(eng):
                run("pe", eng)

            @block.scalar
            def _(eng):
                run("act", eng)

            @block.vector
            def _(eng):
                run("dve", eng)

            @block.gpsimd
            def _(eng):
                run("pool", eng)


CV_GAIN = 0
CV_QG = 48
CV_KG = 49
CV_CW = 50
CV_CB = 82
CV_BA = 90
CV_BX = 98
CV_LAM = 106
NCV = 114


class Builder:
    def __init__(self, n_sub=6):
        self.n_sub = n_sub
        self.nc = bass.Bass("TRN2", target_bir_lowering=False)
        self.P = Prog()
        nc = self.nc
        dt = nc.dram_tensor
        self.xT_d = dt("xT", [D, SEQ], F32, kind="ExternalInput").ap()
        self.cvec_d = dt("cvec", [128, NCV], F32, kind="ExternalInput").ap()
        self.fwin_d = dt("ffn_win", [4, NF, 128, 2048], F32, kind="ExternalInput").ap()
        self.fwout_d = dt("ffn_wout", [4, DFF, D], F32, kind="ExternalInput").ap()
        self.awin_d = dt("attn_win", [8, 128, 3072], F32, kind="ExternalInput").ap()
        self.awout_d = dt("attn_wout", [D, D], F32, kind="ExternalInput").ap()
        self.bias_d = dt("bias_tab", [8, 128, 1280], F32, kind="ExternalInput").ap()
        self.lwin_d = dt("lru_win", [8, 128, 2048], F32, kind="ExternalInput").ap()
        self.lbd_d = dt("lru_bd", [128, 2048], F32, kind="ExternalInput").ap()
        self.lwout_d = dt("lru_wout", [D, D], F32, kind="ExternalInput").ap()
        self.ident_d = dt("ident", [128, 128], F32, kind="ExternalInput").ap()
        self.out_d = dt("outT", [D, SEQ], F32, kind="ExternalOutput").ap()

        total = nc.sbuf_bytes_remaining
        nbytes = (total // 64) * 64 - 64
        self.arena_t = nc.alloc_sbuf_tensor("arena", [128, nbytes // 2], BF16)
        self.arena_bytes = nbytes
        self.off = 0
        self.pall = nc.alloc_psum_tensor("ps", [128, 8 * 512], F32)
        self.psum = [self.pall[:, i * 512:(i + 1) * 512] for i in range(8)]
        self.ps_t = [T("ps%d" % i) for i in range(8)]

        self.xT = self.alloc([128, NCH, SEQ], F32)
        self.x_t = [[T() for _ in range(NT)] for _ in range(NCH)]
        self.hT = self.alloc([128, NCH, SEQ], BF16)
        self.h_t = [[T() for _ in range(NT)] for _ in range(NCH)]
        self.cv = self.alloc([128, NCV], F32)
        self.cv_t = T("cv")
        self.dc = self.alloc([128, 64], F32)
        self.dc_t = T("dc")
        self.ones = self.alloc([128, 128], BF16)
        self.onesA = self.alloc([128, 128], BF16)
        self.onesB = self.alloc([128, 128], BF16)
        self.bones = self.alloc([128, 128], BF16)
        self.ident = self.alloc([128, 128], BF16)
        self.ident_t = T("ident")
        self.const_t = T("consts")
        self.rs = [self.alloc([128, TT], F32) for _ in range(2)]
        self.rs_t = [T(), T()]
        self.NSQ = 2
        self.sq = [self.alloc([128, TT], BF16) for _ in range(self.NSQ)]
        self.sq_t = [T() for _ in range(self.NSQ)]
        self.sq_cnt = 0
        self.rs_cnt = 0
        self.WA = [self.alloc([128, 3072], BF16) for _ in range(3)]
        self.WA_t = [T() for _ in range(3)]
        self.WA_sem = [self.P.new_dma_sem() for _ in range(3)]
        self.NWB = 2 * GMAX
        self.WBflat = self.alloc([128, self.NWB * 1024], BF16)
        self.WBall = self.WBflat.rearrange("p (a b) -> p a b", a=self.NWB)
        self.WB_t = [T() for _ in range(self.NWB)]
        self.WB_sem = [self.P.new_dma_sem() for _ in range(self.NWB)]
        self.WA2_hw_sem = self.P.new_dma_sem()
        self.norm_done = set()
        self.arena_base = self.off
        self.sem_c = self.P.new_dma_sem()
        self.sem_x = [self.P.new_dma_sem() for _ in range(NCH)]
        self.sem_o = self.P.new_dma_sem()
        self.sem_misc = self.P.new_dma_sem()
        self.wa_cnt = 0
        self.out_ops = []

    def alloc(self, shape, dtype):
        esz = 4 if dtype == F32 else 2
        n = 1
        for s in shape[1:]:
            n *= s
        nb = n * esz
        off = self.off
        self.off = (off + nb + 63) // 64 * 64
        assert self.off <= self.arena_bytes, ("SBUF overflow", self.off, self.arena_bytes)
        v = self.arena_t[:, off // 2: off // 2 + nb // 2]
        if dtype == F32:
            v = v.bitcast(F32)
        if len(shape) == 3:
            v = v.rearrange("p (a b) -> p a b", a=shape[1])
        elif len(shape) == 4:
            v = v.rearrange("p (a b c) -> p a b c", a=shape[1], b=shape[2])
        return v

    def phase_begin(self):
        self.off = self.arena_base
        if getattr(self, "pre_norm_lasts", None):
            self.phase_lasts = self.pre_norm_lasts
            self.pre_norm_lasts = None
        else:
            self.phase_lasts = [self.P.ops[e][-1] for e in ("pe", "act", "dve") if self.P.ops[e]]
        for e in ("pe", "act", "dve"):
            self.P.pending[e].extend(self.phase_lasts)

    def mm(self, bank, out_ap, lhsT, rhs, start, stop, reads):
        self.P.op("pe", lambda e: e.matmul(out_ap, lhsT, rhs, start=start, stop=stop),
                  reads=reads, writes=[self.ps_t[bank]])

    def act(self, out, in_, func, reads, writes, scale=None, bias=None):
        kw = {}
        if scale is not None:
            kw["scale"] = scale
        if bias is not None:
            kw["bias"] = bias
        self.P.op("act", lambda e: e.activation(out=out, in_=in_, func=func, **kw), reads=reads, writes=writes)

    def stt(self, out, in0, scalar, in1, op0, op1, reads, writes):
        self.P.op("dve", lambda e: e.scalar_tensor_tensor(out=out, in0=in0, scalar=scalar, in1=in1, op0=op0, op1=op1),
                  reads=reads, writes=writes)

    def tt(self, out, in0, in1, op, reads, writes, eng="dve"):
        self.P.op(eng, lambda e: e.tensor_tensor(out=out, in0=in0, in1=in1, op=op), reads=reads, writes=writes)

    def ts(self, out, in0, s1, s2, op0, op1, reads, writes, eng="dve"):
        if op1 is None:
            self.P.op(eng, lambda e: e.tensor_scalar(out=out, in0=in0, scalar1=s1, scalar2=None, op0=op0),
                      reads=reads, writes=writes)
        else:
            self.P.op(eng, lambda e: e.tensor_scalar(out=out, in0=in0, scalar1=s1, scalar2=s2, op0=op0, op1=op1),
                      reads=reads, writes=writes)

    def cvcol(self, c, n=1):
        return self.cv[:, c:c + n]

    def dccol(self, c, n=1):
        return self.dc[:, c:c + n]

    def init(self):
        P = self.P
        P.dma("sp", lambda e: e.dma_start(out=self.cv, in_=self.cvec_d), self.sem_c, writes=[self.cv_t])
        for t in range(NT):
            grp = []
            for c in range(NCH):
                grp.append(P.dma("sp", lambda e, c=c, t=t: e.dma_start(out=self.xT[:, c, t * TT:(t + 1) * TT],
                                                                        in_=self.xT_d[c * 128:(c + 1) * 128, t * TT:(t + 1) * TT]),
                                 self.sem_x[t], writes=[self.x_t[c][t]]))
            for op in grp:
                op.dma_val = grp[-1].dma_val
        ct = [self.const_t]
        P.op("dve", lambda e: e.memset(self.ones, 1.0), writes=ct)
        P.op("dve", lambda e: e.memset(self.onesA, 0.0), writes=ct)
        P.op("dve", lambda e: e.memset(self.onesA[:, 0:64], 1.0), writes=ct)
        P.op("dve", lambda e: e.memset(self.onesB, 0.0), writes=ct)
        P.op("dve", lambda e: e.memset(self.onesB[:, 64:128], 1.0), writes=ct)
        P.op("dve", lambda e: e.memset(self.bones, 0.0), writes=ct)
        P.op("dve", lambda e: e.memset(self.bones[0:64, 0:64], 1.0), writes=ct)
        P.op("dve", lambda e: e.memset(self.bones[64:128, 64:128], 1.0), writes=ct)
        P.dma("pool", lambda e: e.dma_start(out=self.ident, in_=self.ident_d), self.sem_misc, writes=[self.ident_t])
        rd = [self.cv_t, self.dc_t]
        wr = [self.dc_t]
        lam = self.cvcol(CV_LAM, 8)
        s0 = self.dc[:, 40:48]
        s1 = self.dc[:, 48:56]
        s2 = self.dc[:, 56:64]
        cl = self.dc[:, 8:16]
        self.ts(self.dc[:, 0:1], self.cvcol(CV_KG), 8.0, None, ALU.mult, None, rd, wr)
        self.ts(s1, lam, -1.0, None, ALU.mult, None, rd, wr)
        self.tt(s0, s1, lam, ALU.max, rd, wr)
        self.act(s0, s0, AF.Exp, rd, wr, scale=-1.0)
        self.ts(s1, s0, 1.0, None, ALU.add, None, rd, wr)
        self.act(s2, s1, AF.Ln, rd, wr)
        self.ts(s1, s1, -1.0, 1e-30, ALU.add, ALU.max, rd, wr)
        P.op("dve", lambda e: e.reciprocal(out=s1, in_=s1), reads=rd, writes=wr)
        self.tt(s2, s2, s0, ALU.mult, rd, wr)
        self.tt(s2, s2, s1, ALU.mult, rd, wr)
        self.ts(s0, lam, -1.0, 0.0, ALU.mult, ALU.max, rd, wr)
        self.tt(s2, s2, s0, ALU.add, rd, wr)
        self.ts(cl, s2, -8.0, None, ALU.mult, None, rd, wr)
        self.ts(self.dc[:, 16:24], cl, 0.5, None, ALU.mult, None, rd, wr)
        self.ts(self.dc[:, 24:32], self.cvcol(CV_BA, 8), 0.5, None, ALU.mult, None, rd, wr)
        self.ts(self.dc[:, 32:40], self.cvcol(CV_BX, 8), 0.5, None, ALU.mult, None, rd, wr)

    def rmsnorm_tile(self, n, t):
        if (n, t) in self.norm_done:
            return
        self.norm_done.add((n, t))
        NB_ = 6
        tsl = slice(t * TT, (t + 1) * TT)
        for c in range(NCH):
            k = self.sq_cnt % self.NSQ
            self.sq_cnt += 1
            self.act(self.sq[k], self.xT[:, c, tsl], AF.Square, [self.x_t[c][t]], [self.sq_t[k]])
            self.mm(NB_, self.psum[NB_], self.ones, self.sq[k], c == 0, c == NCH - 1,
                    [self.sq_t[k], self.const_t])
        r = self.rs_cnt % 2
        self.rs_cnt += 1
        self.act(self.rs[r], self.psum[NB_], AF.Ln, [self.ps_t[NB_]], [self.rs_t[r]], scale=1.0 / D, bias=EPS)
        self.act(self.rs[r], self.rs[r], AF.Exp, [self.rs_t[r]], [self.rs_t[r]], scale=-0.5)
        for c in range(NCH):
            self.stt(self.hT[:, c, tsl], self.xT[:, c, tsl], self.cvcol(CV_GAIN + n * 8 + c), self.rs[r],
                     ALU.mult, ALU.mult, [self.x_t[c][t], self.rs_t[r], self.cv_t], [self.h_t[c][t]])

    def rmsnorm(self, n):
        for t in range(NT):
            self.rmsnorm_tile(n, t)

    def tail_norm(self, t):
        nn = self.next_norm
        if nn is None:
            return
        if t == NT - 1:
            self.pre_norm_lasts = [self.P.ops[e][-1] for e in ("pe", "act", "dve") if self.P.ops[e]]
        if t >= 1:
            self.rmsnorm_tile(nn, t - 1)
        if t == NT - 1:
            self.rmsnorm_tile(nn, t)

    def ffn(self, fi, n):
        P = self.P
        self.phase_begin()
        actb = [self.alloc([128, GMAX, SEQ], BF16) for _ in range(2)]
        act_t = [[[T() for _ in range(NT)] for _ in range(GMAX)] for _ in range(2)]
        sg = [self.alloc([128, TT], BF16) for _ in range(2)]
        sg_t = [T(), T()]
        groups = []
        j0 = 0
        for g in GROUPS:
            groups.append(list(range(j0, j0 + g)))
            j0 += g
        wa_slot = {}

        def load_win(j):
            s = self.wa_cnt % 3
            self.wa_cnt += 1
            wa_slot[j] = s
            P.dma("pool", lambda e: e.dma_start(out=self.WA[s][:, 0:2048], in_=self.fwin_d[fi, j], max_dma_last_dim=4096),
                  self.WA_sem[s], writes=[self.WA_t[s]])

        def load_wout(gi):
            for jl, j in enumerate(groups[gi]):
                s = (gi % 2) * GMAX + jl
                P.dma("pool", lambda e, s=s, j=j: e.dma_start(out=self.WBall[:, s, :], in_=self.fwout_d[fi, j * 128:(j + 1) * 128, :],
                                                                max_dma_last_dim=4096),
                      self.WB_sem[s], writes=[self.WB_t[s]])

        for j in range(3):
            load_win(j)
        first_phase = (fi == 0)
        if not first_phase:
            load_wout(0)
            load_wout(1)
        self.rmsnorm(n)
        cnt = [0, 0]

        def UG(gi, jl, j):
            s = wa_slot[j]
            w = self.WA[s][:, 0:2048].rearrange("p (k m) -> p k m", k=8)
            for t in range(NT):
                b = cnt[0] % 2
                cnt[0] += 1
                tsl = slice(t * TT, (t + 1) * TT)
                for kc in range(NCH):
                    self.mm(b, self.psum[b], w[:, kc, 0:128], self.hT[:, kc, tsl], kc == 0, kc == NCH - 1,
                            [self.WA_t[s], self.h_t[kc][t]])
                for kc in range(NCH):
                    self.mm(2 + b, self.psum[2 + b], w[:, kc, 128:256], self.hT[:, kc, tsl], kc == 0, kc == NCH - 1,
                            [self.WA_t[s], self.h_t[kc][t]])
                self.act(sg[b], self.psum[b], AF.Silu, [self.ps_t[b]], [sg_t[b]])
                self.tt(actb[gi % 2][:, jl, tsl], sg[b], self.psum[2 + b], ALU.mult,
                        [sg_t[b], self.ps_t[2 + b]], [act_t[gi % 2][jl][t]])
            if j + 3 < NF:
                load_win(j + 3)
            if first_phase and j < 2:
                load_wout(j)

        def OUT(gi, is_last):
            grp = groups[gi]
            for t in range(NT):
                tsl = slice(t * TT, (t + 1) * TT)
                for c in range(NCH):
                    b = 4 + cnt[1] % 2
                    cnt[1] += 1
                    for jl, j in enumerate(grp):
                        s = (gi % 2) * GMAX + jl
                        self.mm(b, self.psum[b], self.WBall[:, s, c * 128:(c + 1) * 128], actb[gi % 2][:, jl, tsl],
                                jl == 0, jl == len(grp) - 1, [self.WB_t[s], act_t[gi % 2][jl][t]])
                    self.stt(self.xT[:, c, tsl], self.psum[b], 0.5, self.xT[:, c, tsl], ALU.mult, ALU.add,
                             [self.ps_t[b], self.x_t[c][t]], [self.x_t[c][t]])
                    if is_last and self.final_phase:
                        self.store_tile(c, t)
                if is_last:
                    self.tail_norm(t)
            if gi + 2 < len(groups):
                load_wout(gi + 2)

        for gi, grp in enumerate(groups):
            for jl, j in enumerate(grp):
                UG(gi, jl, j)
                if jl == 0 and gi > 0:
                    OUT(gi - 1, False)
        OUT(len(groups) - 1, True)

    def attention(self, n):
        P = self.P
        self.phase_begin()
        attnT = self.alloc([128, NCH, SEQ], BF16)
        at_t = [[T() for _ in range(NT)] for _ in range(NCH)]
        qT = self.alloc([128, SEQ], BF16)
        q_t = [T() for _ in range(NT)]
        kTz = [self.alloc([128, SEQ], BF16) for _ in range(2)]
        k_t = [[T() for _ in range(NT)] for _ in range(2)]
        vz = [self.alloc([128, 16, 128], BF16) for _ in range(2)]
        v_t = [[T() for _ in range(16)] for _ in range(2)]
        expB = self.alloc([128, 2, 640], BF16)
        eb_t = T("expB")
        NE = 3
        E = [self.alloc([128, 2, TT], BF16) for _ in range(NE)]
        E_t = [T() for _ in range(NE)]
        Osb = [self.alloc([128, TT], F32) for _ in range(2)]
        Osb_t = [T(), T()]
        P.op("dve", lambda e: e.memset(kTz[0], 0.0), writes=k_t[0])
        P.op("dve", lambda e: e.memset(kTz[1], 0.0), writes=k_t[1])
        P.op("dve", lambda e: e.memset(vz[0], 0.0), writes=v_t[0])
        P.op("dve", lambda e: e.memset(vz[1], 0.0), writes=v_t[1])

        def load_pair(i):
            s = i % 2
            P.dma("pool", lambda e: e.dma_start(out=self.WA[s], in_=self.awin_d[i], max_dma_last_dim=4096),
                  self.WA_sem[s], writes=[self.WA_t[s]])

        def load_bias(i):
            P.dma("sp", lambda e: e.dma_start(out=self.WA[2][:, 0:2560].bitcast(F32), in_=self.bias_d[i]),
                  self.WA2_hw_sem, writes=[self.WA_t[2]])

        load_pair(0)
        load_bias(0)
        load_pair(1)
        self.rmsnorm(n)
        pc = [0]
        cc = [0, 0]
        qg = self.cvcol(CV_QG)
        kg8 = self.dccol(0)
        for i in range(8):
            s = i % 2
            w = self.WA[s].rearrange("p (k m) -> p k m", k=8)
            wt = self.WA_t[s]
            bsrc = self.WA[2][:, 0:2560].bitcast(F32).rearrange("p (h q) -> p h q", h=2)
            self.act(expB, bsrc, AF.Copy, [self.WA_t[2]], [eb_t])
            P.op("dve", lambda e: e.memset(expB[64:128, :, 0:64], -30000.0), reads=[], writes=[eb_t])
            P.op("dve", lambda e: e.memset(expB[0:64, :, 576:640], -30000.0), reads=[], writes=[eb_t])
            if i + 1 < 8:
                load_bias(i + 1)
            pitems = [(which, t) for which in range(2) for t in range(NT)]
            pbank = {}

            def pstage1(m):
                which, t = pitems[m]
                tsl = slice(t * TT, (t + 1) * TT)
                b = pc[0] % 3
                pc[0] += 1
                pbank[m] = b
                for kc in range(NCH):
                    self.mm(b, self.psum[b], w[:, kc, which * 128:(which + 1) * 128], self.hT[:, kc, tsl],
                            kc == 0, kc == NCH - 1, [wt, self.h_t[kc][t]])
                k = self.sq_cnt % self.NSQ
                self.sq_cnt += 1
                pbank[(m, "sq")] = k
                self.act(self.sq[k], self.psum[b], AF.Square, [self.ps_t[b]], [self.sq_t[k]])

            def pstage2(m):
                which, t = pitems[m]
                tsl = slice(t * TT, (t + 1) * TT)
                b = pbank[m]
                k = pbank[(m, "sq")]
                self.mm(3, self.psum[3], self.bones, self.sq[k], True, True, [self.sq_t[k], self.const_t])
                r = self.rs_cnt % 2
                self.rs_cnt += 1
                self.act(self.rs[r], self.psum[3], AF.Ln, [self.ps_t[3]], [self.rs_t[r]], scale=1.0, bias=64.0 * EPS)
                self.act(self.rs[r], self.rs[r], AF.Exp, [self.rs_t[r]], [self.rs_t[r]], scale=-0.5)
                if which == 0:
                    self.stt(qT[:, tsl], self.psum[b], qg, self.rs[r], ALU.mult, ALU.mult,
                             [self.ps_t[b], self.rs_t[r], self.cv_t], [q_t[t]])
                else:
                    for hh in range(2):
                        ps_ = slice(hh * 64, (hh + 1) * 64)
                        self.stt(kTz[hh][ps_, tsl], self.psum[b][ps_, :], kg8[ps_, :], self.rs[r][ps_, :], ALU.mult, ALU.mult,
                                 [self.ps_t[b], self.rs_t[r], self.dc_t], [k_t[hh][t]])

            for m in range(len(pitems) + 1):
                if m < len(pitems):
                    pstage1(m)
                if m >= 1:
                    pstage2(m - 1)
            for kt in range(16):
                t = kt // 4
                b = pc[0] % 3
                pc[0] += 1
                ksl = slice(kt * 128, (kt + 1) * 128)
                for kc in range(NCH):
                    self.mm(b, self.psum[b][:, 0:128], self.hT[:, kc, ksl], w[:, kc, 256:384], kc == 0, kc == NCH - 1,
                            [wt, self.h_t[kc][t]])
                for hh in range(2):
                    cs = slice(hh * 64, (hh + 1) * 64)
                    P.op("dve", lambda e, hh=hh, kt=kt, cs=cs, b=b: e.tensor_copy(out=vz[hh][:, kt, cs], in_=self.psum[b][:, cs]),
                         reads=[self.ps_t[b]], writes=[v_t[hh][kt]])
            if i + 2 < 8:
                load_pair(i + 2)
            seq = []
            for t in range(NT):
                kts = [kt for kt in (4 * t, 4 * t - 1, 4 * t + 1, 4 * t - 2, 4 * t + 2, 4 * t - 3, 4 * t + 3, 4 * t - 4)
                       if 0 <= kt < 16]
                for idx, kt in enumerate(kts):
                    seq.append((t, kt, idx == 0, idx == len(kts) - 1))
            st = {}

            def cstage1(k):
                t, kt, first, last = seq[k]
                a = max(TT * t, 128 * kt)
                bnd = min(TT * t + TT, 128 * kt + 640)
                nq = bnd - a
                sb = 2 * (cc[1] % 3)
                r = cc[1] % NE
                cc[1] += 1
                st[k] = (a, bnd, nq, r)
                for hh in range(2):
                    self.mm(sb + hh, self.psum[sb + hh][:, 0:nq], kTz[hh][:, kt * 128:(kt + 1) * 128], qT[:, a:bnd], True, False,
                            [k_t[hh][kt // 4], q_t[t]])
                    self.mm(sb + hh, self.psum[sb + hh][:, 0:nq], self.ident, expB[:, hh, a - 128 * kt: bnd - 128 * kt], False, True,
                            [self.ident_t, eb_t])
                sview = self.pall[:, sb * 512:(sb + 2) * 512].rearrange("p (h q) -> p h q", h=2)[:, :, 0:nq]
                self.act(E[r][:, :, 0:nq], sview, AF.Exp, [self.ps_t[sb], self.ps_t[sb + 1]], [E_t[r]])

            def cstage2(k):
                t, kt, first, last = seq[k]
                a, bnd, nq, r = st[k]
                ob = 6
                db = 7
                osl = slice(a - TT * t, bnd - TT * t)
                for hh in range(2):
                    self.mm(ob, self.psum[ob][:, osl], vz[hh][:, kt, :], E[r][:, hh, 0:nq], first and hh == 0, last and hh == 1,
                            [v_t[hh][kt], E_t[r]])
                    self.mm(db, self.psum[db][:, osl], self.onesA if hh == 0 else self.onesB,
                            E[r][:, hh, 0:nq], first and hh == 0, last and hh == 1, [self.const_t, E_t[r]])
                if last:
                    rr = self.rs_cnt % 2
                    self.rs_cnt += 1
                    oo = cc[0] % 2
                    cc[0] += 1
                    self.act(self.rs[rr], self.psum[db], AF.Ln, [self.ps_t[db]], [self.rs_t[rr]])
                    P.op("dve", lambda e: e.tensor_copy(out=Osb[oo], in_=self.psum[ob]), reads=[self.ps_t[ob]], writes=[Osb_t[oo]])
                    self.act(self.rs[rr], self.rs[rr], AF.Exp, [self.rs_t[rr]], [self.rs_t[rr]], scale=-1.0)
                    self.tt(attnT[:, i, t * TT:(t + 1) * TT], Osb[oo], self.rs[rr], ALU.mult,
                            [Osb_t[oo], self.rs_t[rr]], [at_t[i][t]])

            LA = 2
            for k in range(len(seq) + LA):
                if k < len(seq):
                    cstage1(k)
                if k >= LA:
                    cstage2(k - LA)
        self.out_proj(self.awout_d, attnT, at_t)

    def out_proj(self, w_d, yT, y_t, extra=()):
        P = self.P
        for i in range(8):
            P.dma("pool", lambda e, i=i: e.dma_start(out=self.WBall[:, i, :], in_=w_d[i * 128:(i + 1) * 128, :], max_dma_last_dim=4096),
                  self.WB_sem[i], writes=[self.WB_t[i]], extra=extra)
        cnt = 0
        for t in range(NT):
            tsl = slice(t * TT, (t + 1) * TT)
            for c in range(NCH):
                b = cnt % 2
                cnt += 1
                for i in range(8):
                    self.mm(b, self.psum[b], self.WBall[:, i, c * 128:(c + 1) * 128], yT[:, i, tsl], i == 0, i == 7,
                            [self.WB_t[i], y_t[i][t]])
                self.tt(self.xT[:, c, tsl], self.psum[b], self.xT[:, c, tsl], ALU.add,
                        [self.ps_t[b], self.x_t[c][t]], [self.x_t[c][t]])
            self.tail_norm(t)

    def lru(self, n):
        P = self.P
        self.phase_begin()
        H = 2 * TT
        yT = self.alloc([128, NCH, SEQ], BF16)
        y_t = [[T() for _ in range(NT)] for _ in range(NCH)]
        bd = [self.alloc([128, 2, 128], F32) for _ in range(2)]
        bd_t = [T(), T()]
        bd_sem = [P.new_dma_sem(), P.new_dma_sem()]
        halo = [self.alloc([128, 4], F32) for _ in range(2)]
        halo_t = [T(), T()]
        carry = self.alloc([128, 4], F32)
        carry_t = T()
        names = ("xc", "tha", "thx", "Ab", "gw")
        bufs = [{}, {}]
        tts = [{}, {}]
        for j, nm in enumerate(names):
            bufs[0][nm] = self.alloc([128, H], F32)
            bufs[1][nm] = self.WBflat[:, j * 2 * H:(j + 1) * 2 * H].bitcast(F32)
            tts[0][nm] = T()
            tts[1][nm] = T()
        xc3 = [bufs[0]["xc"], bufs[1]["xc"], self.alloc([128, H], F32)]
        xc3_t = [tts[0]["xc"], tts[1]["xc"], T()]
        lbd = self.lbd_d.rearrange("p (c s m) -> p c s m", c=8, s=2)
        xb2 = self.pall[:, 0:2 * TT]
        g2 = self.pall[:, 2 * TT:4 * TT]
        ra2 = self.pall[:, 4 * TT:6 * TT]
        rx2 = self.pall[:, 6 * TT:8 * TT]
        XB, GB, RA, RX = [self.ps_t[0], self.ps_t[1]], [self.ps_t[2], self.ps_t[3]], [self.ps_t[4], self.ps_t[5]], [self.ps_t[6], self.ps_t[7]]

        def load_bd(c, extra=()):
            b = c % 2
            P.dma("sp", lambda e: e.dma_start(out=bd[b], in_=lbd[:, c, :, :]), bd_sem[b], writes=[bd_t[b]], extra=extra)

        def load_w(c):
            s = self.wa_cnt % 3
            self.wa_cnt += 1
            P.dma("pool", lambda e: e.dma_start(out=self.WA[s][:, 0:2048], in_=self.lwin_d[c], max_dma_last_dim=4096),
                  self.WA_sem[s], writes=[self.WA_t[s]])
            return s

        slots = {}
        for c in range(3):
            slots[c] = load_w(c)
        load_bd(0, extra=self.phase_lasts)
        load_bd(1, extra=self.phase_lasts)
        self.rmsnorm(n)
        units = [(c, m) for c in range(NCH) for m in range(2)]
        nu = len(units)
        xbv = [self.pall[:, 0:2 * TT], self.pall[:, 2 * TT:4 * TT]]
        XBT = [[self.ps_t[0], self.ps_t[1]], [self.ps_t[2], self.ps_t[3]]]
        g2 = self.pall[:, 4 * TT:6 * TT]
        GB = [self.ps_t[4], self.ps_t[5]]

        def XB(k):
            c, m = units[k]
            s = slots[c]
            w = self.WA[s][:, 0:2048].rearrange("p (k m) -> p k m", k=8)
            wt = self.WA_t[s]
            pb = 2 * (k % 2)
            for hf in range(2):
                t = 2 * m + hf
                tsl = slice(t * TT, (t + 1) * TT)
                for kc in range(NCH):
                    self.mm(pb + hf, self.psum[pb + hf], w[:, kc, 0:128], self.hT[:, kc, tsl], kc == 0, kc == NCH - 1,
                            [wt, self.h_t[kc][t]])

        def CONV(k):
            c, m = units[k]
            u = k % 2
            pu = (k - 1) % 2
            xb2 = xbv[k % 2]
            XBk = XBT[k % 2]
            cw = [self.cvcol(CV_CW + j * 8 + c) for j in range(4)]
            cb = self.cvcol(CV_CB + c)
            P.op("dve", lambda e: e.tensor_copy(out=halo[u][:, 0:3], in_=xb2[:, H - 3:H]), reads=XBk, writes=[halo_t[u]])
            xc = xc3[k % 3]
            xct = xc3_t[k % 3]
            self.ts(xc, xb2, cw[3], cb, ALU.mult, ALU.add, XBk + [self.cv_t], [xct])
            for j in (2, 1, 0):
                sh = 3 - j
                self.stt(xc[:, sh:H], xb2[:, 0:H - sh], cw[j], xc[:, sh:H], ALU.mult, ALU.add,
                         XBk + [self.cv_t, xct], [xct])
                if m > 0:
                    self.stt(xc[:, 0:sh], halo[pu][:, 3 - sh:3], cw[j], xc[:, 0:sh], ALU.mult, ALU.add,
                             [halo_t[pu], self.cv_t, xct], [xct])

        def G(k):
            c, m = units[k]
            u = k % 2
            B_, Tt = bufs[u], tts[u]
            xc = xc3[k % 3]
            xct = xc3_t[k % 3]
            b = c % 2
            hba = self.dccol(24 + c)
            hbx = self.dccol(32 + c)
            for hf in range(2):
                hs = slice(hf * TT, (hf + 1) * TT)
                self.mm(6, self.psum[6], bd[b][:, 0, :], xc[:, hs], True, True, [bd_t[b], xct])
                self.mm(7, self.psum[7], bd[b][:, 1, :], xc[:, hs], True, True, [bd_t[b], xct])
                self.act(B_["tha"][:, hs], self.psum[6], AF.Tanh, [self.ps_t[6], self.dc_t], [Tt["tha"]], scale=0.5, bias=hba)
                self.act(B_["thx"][:, hs], self.psum[7], AF.Tanh, [self.ps_t[7], self.dc_t], [Tt["thx"]], scale=0.5, bias=hbx)
            if m == 1 and c + 2 < NCH:
                load_bd(c + 2)

        def A2(k):
            c, m = units[k]
            s = slots[c]
            w = self.WA[s][:, 0:2048].rearrange("p (k m) -> p k m", k=8)
            wt = self.WA_t[s]
            for hf in range(2):
                t = 2 * m + hf
                tsl = slice(t * TT, (t + 1) * TT)
                for kc in range(NCH):
                    self.mm(4 + hf, self.psum[4 + hf], w[:, kc, 128:256], self.hT[:, kc, tsl], kc == 0, kc == NCH - 1,
                            [wt, self.h_t[kc][t]])
        def B_rest(k):
            c, m = units[k]
            u = k % 2
            B_, Tt = bufs[u], tts[u]
            cl = self.dccol(8 + c)
            hcl = self.dccol(16 + c)
            tha, Ab, gw = B_["tha"], B_["Ab"], B_["gw"]
            self.act(gw, g2, AF.Square, GB, [Tt["gw"]], scale=0.21145921)
            self.stt(gw, gw, 1.0, g2, ALU.add, ALU.mult, [Tt["gw"]] + GB, [Tt["gw"]])
            self.act(Ab, tha, AF.Exp, [Tt["tha"], self.dc_t], [Tt["Ab"]], scale=hcl, bias=hcl)
            self.act(tha, tha, AF.Exp, [Tt["tha"], self.dc_t], [Tt["tha"]], scale=cl, bias=cl)
            self.act(gw, gw, AF.Tanh, [Tt["gw"]], [Tt["gw"]], scale=0.7978845608)
            self.stt(gw, gw, 1.0, g2, ALU.add, ALU.mult, [Tt["gw"]] + GB, [Tt["gw"]])
            self.act(tha, tha, AF.Sqrt, [Tt["tha"]], [Tt["tha"]], scale=-0.0625, bias=0.0625)

        def C(k):
            c, m = units[k]
            u = k % 2
            B_, Tt = bufs[u], tts[u]
            tha, thx, Ab, gw = B_["tha"], B_["thx"], B_["Ab"], B_["gw"]
            xc = xc3[k % 3]
            xct = xc3_t[k % 3]
            self.ts(thx, thx, 1.0, 1.0, ALU.add, ALU.mult, [Tt["thx"]], [Tt["thx"]], eng="pool")
            self.tt(thx, thx, xc, ALU.mult, [Tt["thx"], xct], [Tt["thx"]], eng="pool")
            self.tt(thx, thx, tha, ALU.mult, [Tt["thx"], Tt["tha"]], [Tt["thx"]], eng="pool")
            if m == 0:
                P.op("dve", lambda e: e.tensor_tensor_scan(out=tha, data0=Ab, data1=thx, initial=0.0,
                                                           op0=ALU.mult, op1=ALU.add),
                     reads=[Tt["Ab"], Tt["thx"]], writes=[Tt["tha"]])
                P.op("dve", lambda e: e.tensor_copy(out=carry[:, 0:1], in_=tha[:, H - 1:H]), reads=[Tt["tha"]], writes=[carry_t])
            else:
                P.op("dve", lambda e: e.tensor_tensor_scan(out=tha, data0=Ab, data1=thx,
                                                           initial=carry[:, 0:1], op0=ALU.mult, op1=ALU.add),
                     reads=[Tt["Ab"], Tt["thx"], carry_t], writes=[Tt["tha"]])
            self.tt(yT[:, c, m * H:(m + 1) * H], gw, tha, ALU.mult, [Tt["gw"], Tt["tha"]],
                    [y_t[c][2 * m], y_t[c][2 * m + 1]], eng="pool")

        XB(0)
        CONV(0)
        G(0)
        A2(0)
        XB(1)
        CONV(1)
        XB(2)
        for k in range(nu):
            B_rest(k)
            if k + 1 < nu:
                G(k + 1)
                A2(k + 1)
            if k + 2 < nu:
                CONV(k + 2)
            if k + 3 < nu:
                XB(k + 3)
            C(k)
            if k + 1 < nu:
                c1, m1 = units[k + 1]
                if m1 == 1 and c1 + 3 < NCH:
                    slots[c1 + 3] = load_w(c1 + 3)
        lasts = [P.ops[e][-1] for e in ("pe", "act", "dve", "pool")]
        self.out_proj(self.lwout_d, yT, y_t, extra=lasts)

    def store_tile(self, c, t):
        tsl = slice(t * TT, (t + 1) * TT)
        op = self.P.dma("sp", lambda e: e.dma_start(out=self.out_d[c * 128:(c + 1) * 128, tsl], in_=self.xT[:, c, tsl]),
                        self.sem_o, reads=[self.x_t[c][t]])
        self.out_ops.append(op)
        self.stored.add((c, t))

    def finish(self):
        P = self.P
        for t in range(NT):
            for c in range(NCH):
                if (c, t) not in self.stored:
                    self.store_tile(c, t)
        for op in self.out_ops:
            op.dma_val = self.out_ops[-1].dma_val
        P.emit(self.nc, final_wait_ops=self.out_ops)

    def build(self):
        self.init()
        subs = [lambda: self.ffn(0, 0), lambda: self.attention(1), lambda: self.ffn(1, 2),
                lambda: self.ffn(2, 3), lambda: self.lru(4), lambda: self.ffn(3, 5)]
        self.stored = set()
        for k in range(self.n_sub):
            self.next_norm = (k + 1) if k + 1 < self.n_sub else None
            self.final_phase = (k == self.n_sub - 1)
            subs[k]()
        self.finish()
        return self.nc


def prep_shared(inp):
    f32 = np.float32
    cvec = np.zeros((128, NCV), f32)
    norms = [inp["norm_ffn_pre"][0], inp["norm_mix"][0], inp["norm_ffn_post"][0],
             inp["norm_ffn_pre"][1], inp["norm_mix"][1], inp["norm_ffn_post"][1]]
    for n, g in enumerate(norms):
        cvec[:, CV_GAIN + n * 8: CV_GAIN + n * 8 + 8] = np.asarray(g, f32).reshape(8, 128).T
    cvec[:, CV_QG] = np.tile(np.asarray(inp["attn_q_gain"][0], f32), 2)
    cvec[:, CV_KG] = np.tile(np.asarray(inp["attn_k_gain"][0], f32), 2)
    cw = np.asarray(inp["lru_conv_w"][0], f32)
    for j in range(4):
        cvec[:, CV_CW + j * 8: CV_CW + j * 8 + 8] = cw[j].reshape(8, 128).T
    cvec[:, CV_CB:CV_CB + 8] = np.asarray(inp["lru_conv_b"][0], f32).reshape(8, 128).T
    cvec[:, CV_BA:CV_BA + 8] = np.asarray(inp["lru_b_a"][0], f32).reshape(8, 128).T
    cvec[:, CV_BX:CV_BX + 8] = np.asarray(inp["lru_b_x"][0], f32).reshape(8, 128).T
    cvec[:, CV_LAM:CV_LAM + 8] = np.asarray(inp["lru_lambda"][0], f32).reshape(8, 128).T

    def win_tiles(w):
        w = np.asarray(w, f32).reshape(8, 128, 2, NF, 128)
        return np.ascontiguousarray(w.transpose(3, 1, 0, 2, 4)).reshape(NF, 128, 2048)

    ffn_win = np.stack([win_tiles(inp["ffn_pre_w_in"][0]), win_tiles(inp["ffn_post_w_in"][0]),
                        win_tiles(inp["ffn_pre_w_in"][1]), win_tiles(inp["ffn_post_w_in"][1])])
    ffn_wout = np.stack([np.asarray(inp["ffn_pre_w_out"][0], f32), np.asarray(inp["ffn_post_w_out"][0], f32),
                         np.asarray(inp["ffn_pre_w_out"][1], f32), np.asarray(inp["ffn_post_w_out"][1], f32)])
    aw = np.asarray(inp["attn_w_in"][0], f32).reshape(8, 128, 3, 8, 128)
    attn_win = np.ascontiguousarray(aw.transpose(3, 1, 0, 2, 4)).reshape(8, 128, 3072)
    rb = np.asarray(inp["attn_rel_bias"][0], f32)
    ko = np.arange(128)[:, None]
    qo = np.arange(640)[None, :]
    rel = np.clip(qo - ko, -128, 128) + 128
    bt = rb[:, rel]
    bias_tab = np.ascontiguousarray(bt.reshape(8, 2, 128, 640).transpose(0, 2, 1, 3)).reshape(8, 128, 1280)
    lw = np.asarray(inp["lru_w_in"][0], f32).reshape(8, 128, 2, 8, 128)
    lru_win = np.ascontiguousarray(lw.transpose(3, 1, 0, 2, 4)).reshape(8, 128, 2048)
    bd = np.zeros((128, 8, 2, 128), f32)
    for si, key in enumerate(("lru_w_a", "lru_w_x")):
        wb = np.asarray(inp[key][0], f32)
        for c in range(8):
            bd[0:64, c, si, 0:64] = wb[2 * c]
            bd[64:128, c, si, 64:128] = wb[2 * c + 1]
    return {
        "cvec": cvec, "ffn_win": ffn_win, "ffn_wout": ffn_wout, "attn_win": attn_win,
        "attn_wout": np.ascontiguousarray(np.asarray(inp["attn_w_out"][0], f32)),
        "bias_tab": bias_tab, "lru_win": lru_win, "lru_bd": bd.reshape(128, 2048),
        "lru_wout": np.ascontiguousarray(np.asarray(inp["lru_w_out"][0], f32)),
        "ident": np.eye(128, dtype=f32),
    }


_NC_CACHE = {}


def get_nc(n_sub=6):
    if n_sub not in _NC_CACHE:
        _NC_CACHE[n_sub] = Builder(n_sub).build()
    return _NC_CACHE[n_sub]


def kernel(**inputs):
    x = np.asarray(inputs["x"], np.float32)
    shared = prep_shared(inputs)
    in_maps = []
    for b in range(NB):
        m = dict(shared)
        m["xT"] = np.ascontiguousarray(x[b].T)
        in_maps.append(m)
    nc = Builder(6).build()
    res = run_bass_kernel_spmd(nc, in_maps, core_ids=list(range(NB)))
    out = np.stack([np.ascontiguousarray(res.results[b]["outT"].T) for b in range(NB)])
    return out.astype(np.float32)
```

```python
import contextlib
import numpy as np
import concourse.bass as bass
import concourse.mybir as mybir
from concourse.bass_utils import run_bass_kernel_spmd

F32 = mybir.dt.float32
BF16 = mybir.dt.bfloat16
AF = mybir.ActivationFunctionType
ALU = mybir.AluOpType

D = 1024
SEQ = 2048
NB = 8
NCH = 8
NT = 4
TT = 512
DFF = 2816
NF = 22
EPS = 1e-6
NHEAD = 16
GROUPS = [5, 5, 4, 4, 4]
GMAX = 5

ENGS = ("pe", "act", "dve", "pool", "sp")


class T:
    __slots__ = ("name", "last_w", "readers")

    def __init__(self, name=""):
        self.name = name
        self.last_w = None
        self.readers = []


class Op:
    __slots__ = ("eng", "fn", "idx", "deps", "needs_inc", "inc_val", "is_dma", "dma_sem_id", "dma_val")

    def __init__(self, eng, fn, is_dma=False):
        self.eng = eng
        self.fn = fn
        self.idx = -1
        self.deps = []
        self.needs_inc = False
        self.inc_val = 0
        self.is_dma = is_dma
        self.dma_sem_id = None
        self.dma_val = 0


class Prog:
    def __init__(self):
        self.ops = {e: [] for e in ENGS}
        self.pending = {e: [] for e in ENGS}
        self.dma_counts = []

    def new_dma_sem(self):
        self.dma_counts.append(0)
        return len(self.dma_counts) - 1

    def _add(self, op, reads, writes):
        deps = []
        for t in reads:
            if t.last_w is not None:
                deps.append(t.last_w)
        for t in writes:
            if t.last_w is not None:
                deps.append(t.last_w)
            deps.extend(t.readers)
        deps.extend(self.pending[op.eng])
        self.pending[op.eng] = []
        seen = set()
        for d in deps:
            if d is op or id(d) in seen:
                continue
            seen.add(id(d))
            op.deps.append(d)
        for t in reads:
            t.readers.append(op)
        for t in writes:
            t.last_w = op
            t.readers = []
        op.idx = len(self.ops[op.eng])
        self.ops[op.eng].append(op)
        return op

    def op(self, eng, fn, reads=(), writes=(), extra=()):
        self.pending[eng].extend(extra)
        return self._add(Op(eng, fn), list(reads), list(writes))

    def dma(self, eng, fn, sem_id, reads=(), writes=(), extra=()):
        op = Op(eng, fn, is_dma=True)
        op.dma_sem_id = sem_id
        self.dma_counts[sem_id] += 16
        op.dma_val = self.dma_counts[sem_id]
        self.pending[eng].extend(extra)
        return self._add(op, list(reads), list(writes))

    def barrier(self, engs=("pe", "act", "dve")):
        lasts = [self.ops[e][-1] for e in engs if self.ops[e]]
        for e in engs:
            self.pending[e].extend(lasts)

    def emit(self, nc, final_wait_ops=()):
        for e in ENGS:
            for op in self.ops[e]:
                for d in op.deps:
                    if not d.is_dma:
                        d.needs_inc = True
        for e in ENGS:
            c = 0
            for op in self.ops[e]:
                if (not op.is_dma) and op.needs_inc:
                    c += 1
                    op.inc_val = c
        nd = len(self.dma_counts)
        with contextlib.ExitStack() as st:
            esem = {e: st.enter_context(nc.semaphore("s_" + e)) for e in ENGS}
            dsem = [st.enter_context(nc.semaphore("d_%d" % i)) for i in range(nd)]
            block = st.enter_context(nc.Block())

            def run(e, eng):
                waited_e = {x: 0 for x in ENGS}
                waited_d = [0] * nd
                for op in self.ops[e]:
                    for d in op.deps:
                        if d.is_dma:
                            if waited_d[d.dma_sem_id] < d.dma_val:
                                eng.wait_ge(dsem[d.dma_sem_id], d.dma_val)
                                waited_d[d.dma_sem_id] = d.dma_val
                        else:
                            if d.eng == e and e == "pe":
                                continue
                            if waited_e[d.eng] < d.inc_val:
                                eng.wait_ge(esem[d.eng], d.inc_val)
                                waited_e[d.eng] = d.inc_val
                    ins = op.fn(eng)
                    if op.is_dma:
                        ins.then_inc(dsem[op.dma_sem_id], 16)
                    elif op.needs_inc:
                        ins.then_inc(esem[e], 1)
                if e == "sp":
                    for op in final_wait_ops:
                        if waited_d[op.dma_sem_id] < op.dma_val:
                            eng.wait_ge(dsem[op.dma_sem_id], op.dma_val)
                            waited_d[op.dma_sem_id] = op.dma_val

            @block.sync
            def _(eng):
                run("sp", eng)

            @block.tensor
            def _(eng):
                run("pe", eng)

            @block.scalar
            def _(eng):
                run("act", eng)

            @block.vector
            def _(eng):
                run("dve", eng)

            @block.gpsimd
            def _(eng):
                run("pool", eng)


CV_GAIN = 0
CV_QG = 48
CV_KG = 49
CV_CW = 50
CV_CB = 82
CV_BA = 90
CV_BX = 98
CV_LAM = 106
NCV = 114


class Builder:
    def __init__(self, n_sub=6):
        self.n_sub = n_sub
        self.nc = bass.Bass("TRN2", target_bir_lowering=False)
        self.P = Prog()
        nc = self.nc
        dt = nc.dram_tensor
        self.xT_d = dt("xT", [D, SEQ], F32, kind="ExternalInput").ap()
        self.cvec_d = dt("cvec", [128, NCV], F32, kind="ExternalInput").ap()
        self.fwin_d = dt("ffn_win", [4, NF, 128, 2048], F32, kind="ExternalInput").ap()
        self.fwout_d = dt("ffn_wout", [4, DFF, D], F32, kind="ExternalInput").ap()
        self.awin_d = dt("attn_win", [8, 128, 3072], F32, kind="ExternalInput").ap()
        self.awout_d = dt("attn_wout", [D, D], F32, kind="ExternalInput").ap()
        self.bias_d = dt("bias_tab", [8, 128, 1280], F32, kind="ExternalInput").ap()
        self.lwin_d = dt("lru_win", [8, 128, 2048], F32, kind="ExternalInput").ap()
        self.lbd_d = dt("lru_bd", [128, 2048], F32, kind="ExternalInput").ap()
        self.lwout_d = dt("lru_wout", [D, D], F32, kind="ExternalInput").ap()
        self.ident_d = dt("ident", [128, 128], F32, kind="ExternalInput").ap()
        self.out_d = dt("outT", [D, SEQ], F32, kind="ExternalOutput").ap()

        total = nc.sbuf_bytes_remaining
        nbytes = (total // 64) * 64 - 64
        self.arena_t = nc.alloc_sbuf_tensor("arena", [128, nbytes // 2], BF16)
        self.arena_bytes = nbytes
        self.off = 0
        self.pall = nc.alloc_psum_tensor("ps", [128, 8 * 512], F32)
        self.psum = [self.pall[:, i * 512:(i + 1) * 512] for i in range(8)]
        self.ps_t = [T("ps%d" % i) for i in range(8)]

        self.xT = self.alloc([128, NCH, SEQ], F32)
        self.x_t = [[T() for _ in range(NT)] for _ in range(NCH)]
        self.hT = self.alloc([128, NCH, SEQ], BF16)
        self.h_t = [[T() for _ in range(NT)] for _ in range(NCH)]
        self.cv = self.alloc([128, NCV], F32)
        self.cv_t = T("cv")
        self.dc = self.alloc([128, 64], F32)
        self.dc_t = T("dc")
        self.ones = self.alloc([128, 128], BF16)
        self.onesA = self.alloc([128, 128], BF16)
        self.onesB = self.alloc([128, 128], BF16)
        self.bones = self.alloc([128, 128], BF16)
        self.ident = self.alloc([128, 128], BF16)
        self.ident_t = T("ident")
        self.const_t = T("consts")
        self.rs = [self.alloc([128, TT], F32) for _ in range(2)]
        self.rs_t = [T(), T()]
        self.NSQ = 2
        self.sq = [self.alloc([128, TT], BF16) for _ in range(self.NSQ)]
        self.sq_t = [T() for _ in range(self.NSQ)]
        self.sq_cnt = 0
        self.rs_cnt = 0
        self.WA = [self.alloc([128, 3072], BF16) for _ in range(3)]
        self.WA_t = [T() for _ in range(3)]
        self.WA_sem = [self.P.new_dma_sem() for _ in range(3)]
        self.NWB = 2 * GMAX
        self.WBflat = self.alloc([128, self.NWB * 1024], BF16)
        self.WBall = self.WBflat.rearrange("p (a b) -> p a b", a=self.NWB)
        self.WB_t = [T() for _ in range(self.NWB)]
        self.WB_sem = [self.P.new_dma_sem() for _ in range(self.NWB)]
        self.WA2_hw_sem = self.P.new_dma_sem()
        self.norm_done = set()
        self.arena_base = self.off
        self.sem_c = self.P.new_dma_sem()
        self.sem_x = [self.P.new_dma_sem() for _ in range(NCH)]
        self.sem_o = self.P.new_dma_sem()
        self.sem_misc = self.P.new_dma_sem()
        self.wa_cnt = 0
        self.out_ops = []

    def alloc(self, shape, dtype):
        esz = 4 if dtype == F32 else 2
        n = 1
        for s in shape[1:]:
            n *= s
        nb = n * esz
        off = self.off
        self.off = (off + nb + 63) // 64 * 64
        assert self.off <= self.arena_bytes, ("SBUF overflow", self.off, self.arena_bytes)
        v = self.arena_t[:, off // 2: off // 2 + nb // 2]
        if dtype == F32:
            v = v.bitcast(F32)
        if len(shape) == 3:
            v = v.rearrange("p (a b) -> p a b", a=shape[1])
        elif len(shape) == 4:
            v = v.rearrange("p (a b c) -> p a b c", a=shape[1], b=shape[2])
        return v

    def phase_begin(self):
        self.off = self.arena_base
        if getattr(self, "pre_norm_lasts", None):
            self.phase_lasts = self.pre_norm_lasts
            self.pre_norm_lasts = None
        else:
            self.phase_lasts = [self.P.ops[e][-1] for e in ("pe", "act", "dve") if self.P.ops[e]]
        for e in ("pe", "act", "dve"):
            self.P.pending[e].extend(self.phase_lasts)

    def mm(self, bank, out_ap, lhsT, rhs, start, stop, reads):
        self.P.op("pe", lambda e: e.matmul(out_ap, lhsT, rhs, start=start, stop=stop),
                  reads=reads, writes=[self.ps_t[bank]])

    def act(self, out, in_, func, reads, writes, scale=None, bias=None):
        kw = {}
        if scale is not None:
            kw["scale"] = scale
        if bias is not None:
            kw["bias"] = bias
        self.P.op("act", lambda e: e.activation(out=out, in_=in_, func=func, **kw), reads=reads, writes=writes)

    def stt(self, out, in0, scalar, in1, op0, op1, reads, writes):
        self.P.op("dve", lambda e: e.scalar_tensor_tensor(out=out, in0=in0, scalar=scalar, in1=in1, op0=op0, op1=op1),
                  reads=reads, writes=writes)

    def tt(self, out, in0, in1, op, reads, writes, eng="dve"):
        self.P.op(eng, lambda e: e.tensor_tensor(out=out, in0=in0, in1=in1, op=op), reads=reads, writes=writes)

    def ts(self, out, in0, s1, s2, op0, op1, reads, writes, eng="dve"):
        if op1 is None:
            self.P.op(eng, lambda e: e.tensor_scalar(out=out, in0=in0, scalar1=s1, scalar2=None, op0=op0),
                      reads=reads, writes=writes)
        else:
            self.P.op(eng, lambda e: e.tensor_scalar(out=out, in0=in0, scalar1=s1, scalar2=s2, op0=op0, op1=op1),
                      reads=reads, writes=writes)

    def cvcol(self, c, n=1):
        return self.cv[:, c:c + n]

    def dccol(self, c, n=1):
        return self.dc[:, c:c + n]

    def init(self):
        P = self.P
        P.dma("sp", lambda e: e.dma_start(out=self.cv, in_=self.cvec_d), self.sem_c, writes=[self.cv_t])
        for t in range(NT):
            grp = []
            for c in range(NCH):
                grp.append(P.dma("sp", lambda e, c=c, t=t: e.dma_start(out=self.xT[:, c, t * TT:(t + 1) * TT],
                                                                        in_=self.xT_d[c * 128:(c + 1) * 128, t * TT:(t + 1) * TT]),
                                 self.sem_x[t], writes=[self.x_t[c][t]]))
            for op in grp:
                op.dma_val = grp[-1].dma_val
        ct = [self.const_t]
        P.op("dve", lambda e: e.memset(self.ones, 1.0), writes=ct)
        P.op("dve", lambda e: e.memset(self.onesA, 0.0), writes=ct)
        P.op("dve", lambda e: e.memset(self.onesA[:, 0:64], 1.0), writes=ct)
        P.op("dve", lambda e: e.memset(self.onesB, 0.0), writes=ct)
        P.op("dve", lambda e: e.memset(self.onesB[:, 64:128], 1.0), writes=ct)
        P.op("dve", lambda e: e.memset(self.bones, 0.0), writes=ct)
        P.op("dve", lambda e: e.memset(self.bones[0:64, 0:64], 1.0), writes=ct)
        P.op("dve", lambda e: e.memset(self.bones[64:128, 64:128], 1.0), writes=ct)
        P.dma("pool", lambda e: e.dma_start(out=self.ident, in_=self.ident_d), self.sem_misc, writes=[self.ident_t])
        rd = [self.cv_t, self.dc_t]
        wr = [self.dc_t]
        lam = self.cvcol(CV_LAM, 8)
        s0 = self.dc[:, 40:48]
        s1 = self.dc[:, 48:56]
        s2 = self.dc[:, 56:64]
        cl = self.dc[:, 8:16]
        self.ts(self.dc[:, 0:1], self.cvcol(CV_KG), 8.0, None, ALU.mult, None, rd, wr)
        self.ts(s1, lam, -1.0, None, ALU.mult, None, rd, wr)
        self.tt(s0, s1, lam, ALU.max, rd, wr)
        self.act(s0, s0, AF.Exp, rd, wr, scale=-1.0)
        self.ts(s1, s0, 1.0, None, ALU.add, None, rd, wr)
        self.act(s2, s1, AF.Ln, rd, wr)
        self.ts(s1, s1, -1.0, 1e-30, ALU.add, ALU.max, rd, wr)
        P.op("dve", lambda e: e.reciprocal(out=s1, in_=s1), reads=rd, writes=wr)
        self.tt(s2, s2, s0, ALU.mult, rd, wr)
        self.tt(s2, s2, s1, ALU.mult, rd, wr)
        self.ts(s0, lam, -1.0, 0.0, ALU.mult, ALU.max, rd, wr)
        self.tt(s2, s2, s0, ALU.add, rd, wr)
        self.ts(cl, s2, -8.0, None, ALU.mult, None, rd, wr)
        self.ts(self.dc[:, 16:24], cl, 0.5, None, ALU.mult, None, rd, wr)
        self.ts(self.dc[:, 24:32], self.cvcol(CV_BA, 8), 0.5, None, ALU.mult, None, rd, wr)
        self.ts(self.dc[:, 32:40], self.cvcol(CV_BX, 8), 0.5, None, ALU.mult, None, rd, wr)

    def rmsnorm_tile(self, n, t):
        if (n, t) in self.norm_done:
            return
        self.norm_done.add((n, t))
        NB_ = 6
        tsl = slice(t * TT, (t + 1) * TT)
        for c in range(NCH):
            k = self.sq_cnt % self.NSQ
            self.sq_cnt += 1
            self.act(self.sq[k], self.xT[:, c, tsl], AF.Square, [self.x_t[c][t]], [self.sq_t[k]])
            self.mm(NB_, self.psum[NB_], self.ones, self.sq[k], c == 0, c == NCH - 1,
                    [self.sq_t[k], self.const_t])
        r = self.rs_cnt % 2
        self.rs_cnt += 1
        self.act(self.rs[r], self.psum[NB_], AF.Ln, [self.ps_t[NB_]], [self.rs_t[r]], scale=1.0 / D, bias=EPS)
        RB = 7
        self.act(self.psum[RB], self.rs[r], AF.Exp, [self.rs_t[r]], [self.ps_t[RB]], scale=-0.5)
        for c in range(NCH):
            self.stt(self.hT[:, c, tsl], self.xT[:, c, tsl], self.cvcol(CV_GAIN + n * 8 + c), self.psum[RB],
                     ALU.mult, ALU.mult, [self.x_t[c][t], self.ps_t[RB], self.cv_t], [self.h_t[c][t]])

    def rmsnorm(self, n):
        for t in range(NT):
            self.rmsnorm_tile(n, t)

    def tail_norm(self, t):
        nn = self.next_norm
        if nn is None:
            return
        if t == NT - 1:
            self.pre_norm_lasts = [self.P.ops[e][-1] for e in ("pe", "act", "dve") if self.P.ops[e]]
        if t >= 1:
            self.rmsnorm_tile(nn, t - 1)
        if t == NT - 1:
            self.rmsnorm_tile(nn, t)

    def ffn(self, fi, n):
        P = self.P
        self.phase_begin()
        actb = [self.alloc([128, GMAX, SEQ], BF16) for _ in range(2)]
        act_t = [[[T() for _ in range(NT)] for _ in range(GMAX)] for _ in range(2)]
        sg = [self.alloc([128, TT], BF16) for _ in range(2)]
        sg_t = [T(), T()]
        groups = []
        j0 = 0
        for g in GROUPS:
            groups.append(list(range(j0, j0 + g)))
            j0 += g
        wa_slot = {}

        def load_win(j):
            s = self.wa_cnt % 3
            self.wa_cnt += 1
            wa_slot[j] = s
            P.dma("pool", lambda e: e.dma_start(out=self.WA[s][:, 0:2048], in_=self.fwin_d[fi, j], max_dma_last_dim=4096),
                  self.WA_sem[s], writes=[self.WA_t[s]])

        def load_wout(gi):
            for jl, j in enumerate(groups[gi]):
                s = (gi % 2) * GMAX + jl
                P.dma("pool", lambda e, s=s, j=j: e.dma_start(out=self.WBall[:, s, :], in_=self.fwout_d[fi, j * 128:(j + 1) * 128, :],
                                                                max_dma_last_dim=4096),
                      self.WB_sem[s], writes=[self.WB_t[s]])

        for j in range(3):
            load_win(j)
        first_phase = (fi == 0)
        if not first_phase:
            load_wout(0)
            load_wout(1)
        self.rmsnorm(n)
        cnt = [0, 0]

        def UG(gi, jl, j):
            s = wa_slot[j]
            w = self.WA[s][:, 0:2048].rearrange("p (k m) -> p k m", k=8)
            for t in range(NT):
                b = cnt[0] % 2
                cnt[0] += 1
                tsl = slice(t * TT, (t + 1) * TT)
                for kc in range(NCH):
                    self.mm(b, self.psum[b], w[:, kc, 0:128], self.hT[:, kc, tsl], kc == 0, kc == NCH - 1,
                            [self.WA_t[s], self.h_t[kc][t]])
                for kc in range(NCH):
                    self.mm(2 + b, self.psum[2 + b], w[:, kc, 128:256], self.hT[:, kc, tsl], kc == 0, kc == NCH - 1,
                            [self.WA_t[s], self.h_t[kc][t]])
                self.act(sg[b], self.psum[b], AF.Silu, [self.ps_t[b]], [sg_t[b]])
                self.tt(actb[gi % 2][:, jl, tsl], sg[b], self.psum[2 + b], ALU.mult,
                        [sg_t[b], self.ps_t[2 + b]], [act_t[gi % 2][jl][t]])
            if j + 3 < NF:
                load_win(j + 3)
            if first_phase and j < 2:
                load_wout(j)

        def OUT(gi, is_last):
            grp = groups[gi]
            for t in range(NT):
                tsl = slice(t * TT, (t + 1) * TT)
                for c in range(NCH):
                    b = 4 + cnt[1] % 2
                    cnt[1] += 1
                    for jl, j in enumerate(grp):
                        s = (gi % 2) * GMAX + jl
                        self.mm(b, self.psum[b], self.WBall[:, s, c * 128:(c + 1) * 128], actb[gi % 2][:, jl, tsl],
                                jl == 0, jl == len(grp) - 1, [self.WB_t[s], act_t[gi % 2][jl][t]])
                    self.stt(self.xT[:, c, tsl], self.psum[b], 0.5, self.xT[:, c, tsl], ALU.mult, ALU.add,
                             [self.ps_t[b], self.x_t[c][t]], [self.x_t[c][t]])
                    if is_last and self.final_phase:
                        self.store_tile(c, t)
                if is_last:
                    self.tail_norm(t)
            if gi + 2 < len(groups):
                load_wout(gi + 2)

        for gi, grp in enumerate(groups):
            for jl, j in enumerate(grp):
                UG(gi, jl, j)
                if jl == 0 and gi > 0:
                    OUT(gi - 1, False)
        OUT(len(groups) - 1, True)

    def attention(self, n):
        P = self.P
        self.phase_begin()
        attnT = self.alloc([128, NCH, SEQ], BF16)
        at_t = [[T() for _ in range(NT)] for _ in range(NCH)]
        qT = self.alloc([128, SEQ], BF16)
        q_t = [T() for _ in range(NT)]
        kTz = [self.alloc([128, SEQ], BF16) for _ in range(2)]
        k_t = [[T() for _ in range(NT)] for _ in range(2)]
        vz = [self.alloc([128, 16, 128], BF16) for _ in range(2)]
        v_t = [[T() for _ in range(16)] for _ in range(2)]
        expB = self.alloc([128, 2, 640], BF16)
        eb_t = T("expB")
        NE = 3
        E = [self.alloc([128, 2, TT], BF16) for _ in range(NE)]
        E_t = [T() for _ in range(NE)]
        Osb = [self.alloc([128, TT], F32) for _ in range(2)]
        Osb_t = [T(), T()]
        P.op("pool", lambda e: e.memset(kTz[0], 0.0), writes=k_t[0], extra=self.phase_lasts)
        P.op("pool", lambda e: e.memset(kTz[1], 0.0), writes=k_t[1])
        P.op("pool", lambda e: e.memset(vz[0], 0.0), writes=v_t[0])
        P.op("pool", lambda e: e.memset(vz[1], 0.0), writes=v_t[1])

        def load_pair(i):
            s = i % 2
            P.dma("pool", lambda e: e.dma_start(out=self.WA[s], in_=self.awin_d[i], max_dma_last_dim=4096),
                  self.WA_sem[s], writes=[self.WA_t[s]])

        def load_bias(i):
            P.dma("sp", lambda e: e.dma_start(out=self.WA[2][:, 0:2560].bitcast(F32), in_=self.bias_d[i]),
                  self.WA2_hw_sem, writes=[self.WA_t[2]])

        load_pair(0)
        load_bias(0)
        load_pair(1)
        self.rmsnorm(n)
        pc = [0]
        cc = [0, 0]
        qg = self.cvcol(CV_QG)
        kg8 = self.dccol(0)
        for i in range(8):
            s = i % 2
            w = self.WA[s].rearrange("p (k m) -> p k m", k=8)
            wt = self.WA_t[s]
            bsrc = self.WA[2][:, 0:2560].bitcast(F32).rearrange("p (h q) -> p h q", h=2)
            self.act(expB, bsrc, AF.Copy, [self.WA_t[2]], [eb_t])
            P.op("dve", lambda e: e.memset(expB[64:128, :, 0:64], -30000.0), reads=[], writes=[eb_t])
            P.op("dve", lambda e: e.memset(expB[0:64, :, 576:640], -30000.0), reads=[], writes=[eb_t])
            if i + 1 < 8:
                load_bias(i + 1)
            pitems = [(which, t) for which in range(2) for t in range(NT)]
            pbank = {}

            def pstage1(m):
                which, t = pitems[m]
                tsl = slice(t * TT, (t + 1) * TT)
                b = pc[0] % 3
                pc[0] += 1
                pbank[m] = b
                for kc in range(NCH):
                    self.mm(b, self.psum[b], w[:, kc, which * 128:(which + 1) * 128], self.hT[:, kc, tsl],
                            kc == 0, kc == NCH - 1, [wt, self.h_t[kc][t]])
                k = self.sq_cnt % self.NSQ
                self.sq_cnt += 1
                pbank[(m, "sq")] = k
                self.act(self.sq[k], self.psum[b], AF.Square, [self.ps_t[b]], [self.sq_t[k]])

            def pstage2(m):
                which, t = pitems[m]
                tsl = slice(t * TT, (t + 1) * TT)
                b = pbank[m]
                k = pbank[(m, "sq")]
                self.mm(3, self.psum[3], self.bones, self.sq[k], True, True, [self.sq_t[k], self.const_t])
                r = self.rs_cnt % 2
                self.rs_cnt += 1
                self.act(self.rs[r], self.psum[3], AF.Ln, [self.ps_t[3]], [self.rs_t[r]], scale=1.0, bias=64.0 * EPS)
                self.act(self.rs[r], self.rs[r], AF.Exp, [self.rs_t[r]], [self.rs_t[r]], scale=-0.5)
                if which == 0:
                    self.stt(qT[:, tsl], self.psum[b], qg, self.rs[r], ALU.mult, ALU.mult,
                             [self.ps_t[b], self.rs_t[r], self.cv_t], [q_t[t]])
                else:
                    for hh in range(2):
                        ps_ = slice(hh * 64, (hh + 1) * 64)
                        self.stt(kTz[hh][ps_, tsl], self.psum[b][ps_, :], kg8[ps_, :], self.rs[r][ps_, :], ALU.mult, ALU.mult,
                                 [self.ps_t[b], self.rs_t[r], self.dc_t], [k_t[hh][t]])

            for m in range(len(pitems) + 1):
                if m < len(pitems):
                    pstage1(m)
                if m >= 1:
                    pstage2(m - 1)
            for kt in range(16):
                t = kt // 4
                b = pc[0] % 3
                pc[0] += 1
                ksl = slice(kt * 128, (kt + 1) * 128)
                for kc in range(NCH):
                    self.mm(b, self.psum[b][:, 0:128], self.hT[:, kc, ksl], w[:, kc, 256:384], kc == 0, kc == NCH - 1,
                            [wt, self.h_t[kc][t]])
                for hh in range(2):
                    cs = slice(hh * 64, (hh + 1) * 64)
                    P.op("dve", lambda e, hh=hh, kt=kt, cs=cs, b=b: e.tensor_copy(out=vz[hh][:, kt, cs], in_=self.psum[b][:, cs]),
                         reads=[self.ps_t[b]], writes=[v_t[hh][kt]])
            if i + 2 < 8:
                load_pair(i + 2)
            seq = []
            for t in range(NT):
                kts = [kt for kt in (4 * t, 4 * t - 1, 4 * t + 1, 4 * t - 2, 4 * t + 2, 4 * t - 3, 4 * t + 3, 4 * t - 4)
                       if 0 <= kt < 16]
                for idx, kt in enumerate(kts):
                    seq.append((t, kt, idx == 0, idx == len(kts) - 1))
            st = {}

            def cstage1(k):
                t, kt, first, last = seq[k]
                a = max(TT * t, 128 * kt)
                bnd = min(TT * t + TT, 128 * kt + 640)
                nq = bnd - a
                sb = 2 * (cc[1] % 3)
                r = cc[1] % NE
                cc[1] += 1
                st[k] = (a, bnd, nq, r)
                for hh in range(2):
                    self.mm(sb + hh, self.psum[sb + hh][:, 0:nq], kTz[hh][:, kt * 128:(kt + 1) * 128], qT[:, a:bnd], True, False,
                            [k_t[hh][kt // 4], q_t[t]])
                    self.mm(sb + hh, self.psum[sb + hh][:, 0:nq], self.ident, expB[:, hh, a - 128 * kt: bnd - 128 * kt], False, True,
                            [self.ident_t, eb_t])
                sview = self.pall[:, sb * 512:(sb + 2) * 512].rearrange("p (h q) -> p h q", h=2)[:, :, 0:nq]
                self.act(E[r][:, :, 0:nq], sview, AF.Exp, [self.ps_t[sb], self.ps_t[sb + 1]], [E_t[r]])

            def cstage2(k):
                t, kt, first, last = seq[k]
                a, bnd, nq, r = st[k]
                ob = 6
                db = 7
                osl = slice(a - TT * t, bnd - TT * t)
                for hh in range(2):
                    self.mm(ob, self.psum[ob][:, osl], vz[hh][:, kt, :], E[r][:, hh, 0:nq], first and hh == 0, last and hh == 1,
                            [v_t[hh][kt], E_t[r]])
                    self.mm(db, self.psum[db][:, osl], self.onesA if hh == 0 else self.onesB,
                            E[r][:, hh, 0:nq], first and hh == 0, last and hh == 1, [self.const_t, E_t[r]])
                if last:
                    rr = self.rs_cnt % 2
                    self.rs_cnt += 1
                    oo = cc[0] % 2
                    cc[0] += 1
                    self.act(self.rs[rr], self.psum[db], AF.Ln, [self.ps_t[db]], [self.rs_t[rr]])
                    P.op("dve", lambda e: e.tensor_copy(out=Osb[oo], in_=self.psum[ob]), reads=[self.ps_t[ob]], writes=[Osb_t[oo]])
                    self.act(self.rs[rr], self.rs[rr], AF.Exp, [self.rs_t[rr]], [self.rs_t[rr]], scale=-1.0)
                    self.tt(attnT[:, i, t * TT:(t + 1) * TT], Osb[oo], self.rs[rr], ALU.mult,
                            [Osb_t[oo], self.rs_t[rr]], [at_t[i][t]])

            LA = 2
            for k in range(len(seq) + LA):
                if k < len(seq):
                    cstage1(k)
                if k >= LA:
                    cstage2(k - LA)
        self.out_proj(self.awout_d, attnT, at_t)

    def out_proj(self, w_d, yT, y_t, extra=()):
        P = self.P
        for i in range(8):
            P.dma("pool", lambda e, i=i: e.dma_start(out=self.WBall[:, i, :], in_=w_d[i * 128:(i + 1) * 128, :], max_dma_last_dim=4096),
                  self.WB_sem[i], writes=[self.WB_t[i]], extra=extra)
        cnt = 0
        for t in range(NT):
            tsl = slice(t * TT, (t + 1) * TT)
            for c in range(NCH):
                b = cnt % 2
                cnt += 1
                for i in range(8):
                    self.mm(b, self.psum[b], self.WBall[:, i, c * 128:(c + 1) * 128], yT[:, i, tsl], i == 0, i == 7,
                            [self.WB_t[i], y_t[i][t]])
                self.tt(self.xT[:, c, tsl], self.psum[b], self.xT[:, c, tsl], ALU.add,
                        [self.ps_t[b], self.x_t[c][t]], [self.x_t[c][t]])
            self.tail_norm(t)

    def lru(self, n):
        P = self.P
        self.phase_begin()
        H = 2 * TT
        yT = self.alloc([128, NCH, SEQ], BF16)
        y_t = [[T() for _ in range(NT)] for _ in range(NCH)]
        bd = [self.alloc([128, 2, 128], F32) for _ in range(2)]
        bd_t = [T(), T()]
        bd_sem = [P.new_dma_sem(), P.new_dma_sem()]
        halo = [self.alloc([128, 4], F32) for _ in range(2)]
        halo_t = [T(), T()]
        carry = self.alloc([128, 4], F32)
        carry_t = T()
        names = ("xc", "tha", "thx", "Ab", "gw")
        bufs = [{}, {}]
        tts = [{}, {}]
        for j, nm in enumerate(names):
            bufs[0][nm] = self.alloc([128, H], F32)
            bufs[1][nm] = self.WBflat[:, j * 2 * H:(j + 1) * 2 * H].bitcast(F32)
            tts[0][nm] = T()
            tts[1][nm] = T()
        xc3 = [bufs[0]["xc"], bufs[1]["xc"], self.alloc([128, H], F32)]
        xc3_t = [tts[0]["xc"], tts[1]["xc"], T()]
        lbd = self.lbd_d.rearrange("p (c s m) -> p c s m", c=8, s=2)
        xb2 = self.pall[:, 0:2 * TT]
        g2 = self.pall[:, 2 * TT:4 * TT]
        ra2 = self.pall[:, 4 * TT:6 * TT]
        rx2 = self.pall[:, 6 * TT:8 * TT]
        XB, GB, RA, RX = [self.ps_t[0], self.ps_t[1]], [self.ps_t[2], self.ps_t[3]], [self.ps_t[4], self.ps_t[5]], [self.ps_t[6], self.ps_t[7]]

        def load_bd(c, extra=()):
            b = c % 2
            P.dma("sp", lambda e: e.dma_start(out=bd[b], in_=lbd[:, c, :, :]), bd_sem[b], writes=[bd_t[b]], extra=extra)

        def load_w(c):
            s = self.wa_cnt % 3
            self.wa_cnt += 1
            P.dma("pool", lambda e: e.dma_start(out=self.WA[s][:, 0:2048], in_=self.lwin_d[c], max_dma_last_dim=4096),
                  self.WA_sem[s], writes=[self.WA_t[s]])
            return s

        slots = {}
        for c in range(3):
            slots[c] = load_w(c)
        load_bd(0, extra=self.phase_lasts)
        load_bd(1, extra=self.phase_lasts)
        self.rmsnorm(n)
        units = [(c, m) for c in range(NCH) for m in range(2)]
        nu = len(units)
        xbv = [self.pall[:, 0:2 * TT], self.pall[:, 2 * TT:4 * TT]]
        XBT = [[self.ps_t[0], self.ps_t[1]], [self.ps_t[2], self.ps_t[3]]]
        g2 = self.pall[:, 4 * TT:6 * TT]
        GB = [self.ps_t[4], self.ps_t[5]]

        def XB(k):
            c, m = units[k]
            s = slots[c]
            w = self.WA[s][:, 0:2048].rearrange("p (k m) -> p k m", k=8)
            wt = self.WA_t[s]
            pb = 2 * (k % 2)
            for hf in range(2):
                t = 2 * m + hf
                tsl = slice(t * TT, (t + 1) * TT)
                for kc in range(NCH):
                    self.mm(pb + hf, self.psum[pb + hf], w[:, kc, 0:128], self.hT[:, kc, tsl], kc == 0, kc == NCH - 1,
                            [wt, self.h_t[kc][t]])

        def CONV(k):
            c, m = units[k]
            u = k % 2
            pu = (k - 1) % 2
            xb2 = xbv[k % 2]
            XBk = XBT[k % 2]
            cw = [self.cvcol(CV_CW + j * 8 + c) for j in range(4)]
            cb = self.cvcol(CV_CB + c)
            P.op("dve", lambda e: e.tensor_copy(out=halo[u][:, 0:3], in_=xb2[:, H - 3:H]), reads=XBk, writes=[halo_t[u]])
            xc = xc3[k % 3]
            xct = xc3_t[k % 3]
            self.ts(xc, xb2, cw[3], cb, ALU.mult, ALU.add, XBk + [self.cv_t], [xct])
            for j in (2, 1, 0):
                sh = 3 - j
                self.stt(xc[:, sh:H], xb2[:, 0:H - sh], cw[j], xc[:, sh:H], ALU.mult, ALU.add,
                         XBk + [self.cv_t, xct], [xct])
                if m > 0:
                    self.stt(xc[:, 0:sh], halo[pu][:, 3 - sh:3], cw[j], xc[:, 0:sh], ALU.mult, ALU.add,
                             [halo_t[pu], self.cv_t, xct], [xct])

        def G(k):
            c, m = units[k]
            u = k % 2
            B_, Tt = bufs[u], tts[u]
            xc = xc3[k % 3]
            xct = xc3_t[k % 3]
            b = c % 2
            hba = self.dccol(24 + c)
            hbx = self.dccol(32 + c)
            for hf in range(2):
                hs = slice(hf * TT, (hf + 1) * TT)
                self.mm(6, self.psum[6], bd[b][:, 0, :], xc[:, hs], True, True, [bd_t[b], xct])
                self.mm(7, self.psum[7], bd[b][:, 1, :], xc[:, hs], True, True, [bd_t[b], xct])
                self.act(B_["tha"][:, hs], self.psum[6], AF.Tanh, [self.ps_t[6], self.dc_t], [Tt["tha"]], scale=0.5, bias=hba)
                self.act(B_["thx"][:, hs], self.psum[7], AF.Tanh, [self.ps_t[7], self.dc_t], [Tt["thx"]], scale=0.5, bias=hbx)
            if m == 1 and c + 2 < NCH:
                load_bd(c + 2)

        def A2(k):
            c, m = units[k]
            s = slots[c]
            w = self.WA[s][:, 0:2048].rearrange("p (k m) -> p k m", k=8)
            wt = self.WA_t[s]
            for hf in range(2):
                t = 2 * m + hf
                tsl = slice(t * TT, (t + 1) * TT)
                for kc in range(NCH):
                    self.mm(4 + hf, self.psum[4 + hf], w[:, kc, 128:256], self.hT[:, kc, tsl], kc == 0, kc == NCH - 1,
                            [wt, self.h_t[kc][t]])
        def B_rest(k):
            c, m = units[k]
            u = k % 2
            B_, Tt = bufs[u], tts[u]
            cl = self.dccol(8 + c)
            hcl = self.dccol(16 + c)
            tha, Ab, gw = B_["tha"], B_["Ab"], B_["gw"]
            self.act(Ab, tha, AF.Exp, [Tt["tha"], self.dc_t], [Tt["Ab"]], scale=hcl, bias=hcl)
            self.act(tha, tha, AF.Exp, [Tt["tha"], self.dc_t], [Tt["tha"]], scale=cl, bias=cl)
            self.ts(tha, tha, 1.0, None, ALU.min, None, [Tt["tha"]], [Tt["tha"]])
            self.act(gw, g2, AF.Square, GB, [Tt["gw"]], scale=0.21145921)
            self.stt(gw, gw, 1.0, g2, ALU.add, ALU.mult, [Tt["gw"]] + GB, [Tt["gw"]])
            self.act(gw, gw, AF.Tanh, [Tt["gw"]], [Tt["gw"]], scale=0.7978845608)
            self.stt(gw, gw, 1.0, g2, ALU.add, ALU.mult, [Tt["gw"]] + GB, [Tt["gw"]])
            self.act(tha, tha, AF.Sqrt, [Tt["tha"]], [Tt["tha"]], scale=-0.0625, bias=0.0625)

        def C(k):
            c, m = units[k]
            u = k % 2
            B_, Tt = bufs[u], tts[u]
            tha, thx, Ab, gw = B_["tha"], B_["thx"], B_["Ab"], B_["gw"]
            xc = xc3[k % 3]
            xct = xc3_t[k % 3]
            self.ts(thx, thx, 1.0, 1.0, ALU.add, ALU.mult, [Tt["thx"]], [Tt["thx"]], eng="pool")
            self.tt(thx, thx, xc, ALU.mult, [Tt["thx"], xct], [Tt["thx"]], eng="pool")
            self.tt(thx, thx, tha, ALU.mult, [Tt["thx"], Tt["tha"]], [Tt["thx"]], eng="pool")
            if m == 0:
                P.op("dve", lambda e: e.tensor_tensor_scan(out=tha, data0=Ab, data1=thx, initial=0.0,
                                                           op0=ALU.mult, op1=ALU.add),
                     reads=[Tt["Ab"], Tt["thx"]], writes=[Tt["tha"]])
                P.op("dve", lambda e: e.tensor_copy(out=carry[:, 0:1], in_=tha[:, H - 1:H]), reads=[Tt["tha"]], writes=[carry_t])
            else:
                P.op("dve", lambda e: e.tensor_tensor_scan(out=tha, data0=Ab, data1=thx,
                                                           initial=carry[:, 0:1], op0=ALU.mult, op1=ALU.add),
                     reads=[Tt["Ab"], Tt["thx"], carry_t], writes=[Tt["tha"]])
            self.tt(yT[:, c, m * H:(m + 1) * H], gw, tha, ALU.mult, [Tt["gw"], Tt["tha"]],
                    [y_t[c][2 * m], y_t[c][2 * m + 1]], eng="pool")

        XB(0)
        CONV(0)
        G(0)
        A2(0)
        XB(1)
        CONV(1)
        XB(2)
        for k in range(nu):
            B_rest(k)
            if k + 1 < nu:
                G(k + 1)
                A2(k + 1)
            if k + 2 < nu:
                CONV(k + 2)
            if k + 3 < nu:
                XB(k + 3)
            C(k)
            if k + 1 < nu:
                c1, m1 = units[k + 1]
                if m1 == 1 and c1 + 3 < NCH:
                    slots[c1 + 3] = load_w(c1 + 3)
        lasts = [P.ops[e][-1] for e in ("pe", "act", "dve", "pool")]
        self.out_proj(self.lwout_d, yT, y_t, extra=lasts)

    def store_tile(self, c, t):
        tsl = slice(t * TT, (t + 1) * TT)
        op = self.P.dma("sp", lambda e: e.dma_start(out=self.out_d[c * 128:(c + 1) * 128, tsl], in_=self.xT[:, c, tsl]),
                        self.sem_o, reads=[self.x_t[c][t]])
        self.out_ops.append(op)
        self.stored.add((c, t))

    def finish(self):
        P = self.P
        for t in range(NT):
            for c in range(NCH):
                if (c, t) not in self.stored:
                    self.store_tile(c, t)
        for op in self.out_ops:
            op.dma_val = self.out_ops[-1].dma_val
        P.emit(self.nc, final_wait_ops=self.out_ops)

    def build(self):
        self.init()
        subs = [lambda: self.ffn(0, 0), lambda: self.attention(1), lambda: self.ffn(1, 2),
                lambda: self.ffn(2, 3), lambda: self.lru(4), lambda: self.ffn(3, 5)]
        self.stored = set()
        for k in range(self.n_sub):
            self.next_norm = (k + 1) if k + 1 < self.n_sub else None
            self.final_phase = (k == self.n_sub - 1)
            subs[k]()
        self.finish()
        return self.nc


def prep_shared(inp):
    f32 = np.float32
    cvec = np.zeros((128, NCV), f32)
    norms = [inp["norm_ffn_pre"][0], inp["norm_mix"][0], inp["norm_ffn_post"][0],
             inp["norm_ffn_pre"][1], inp["norm_mix"][1], inp["norm_ffn_post"][1]]
    for n, g in enumerate(norms):
        cvec[:, CV_GAIN + n * 8: CV_GAIN + n * 8 + 8] = np.asarray(g, f32).reshape(8, 128).T
    cvec[:, CV_QG] = np.tile(np.asarray(inp["attn_q_gain"][0], f32), 2)
    cvec[:, CV_KG] = np.tile(np.asarray(inp["attn_k_gain"][0], f32), 2)
    cw = np.asarray(inp["lru_conv_w"][0], f32)
    for j in range(4):
        cvec[:, CV_CW + j * 8: CV_CW + j * 8 + 8] = cw[j].reshape(8, 128).T
    cvec[:, CV_CB:CV_CB + 8] = np.asarray(inp["lru_conv_b"][0], f32).reshape(8, 128).T
    cvec[:, CV_BA:CV_BA + 8] = np.asarray(inp["lru_b_a"][0], f32).reshape(8, 128).T
    cvec[:, CV_BX:CV_BX + 8] = np.asarray(inp["lru_b_x"][0], f32).reshape(8, 128).T
    cvec[:, CV_LAM:CV_LAM + 8] = np.asarray(inp["lru_lambda"][0], f32).reshape(8, 128).T

    def win_tiles(w):
        w = np.asarray(w, f32).reshape(8, 128, 2, NF, 128)
        return np.ascontiguousarray(w.transpose(3, 1, 0, 2, 4)).reshape(NF, 128, 2048)

    ffn_win = np.stack([win_tiles(inp["ffn_pre_w_in"][0]), win_tiles(inp["ffn_post_w_in"][0]),
                        win_tiles(inp["ffn_pre_w_in"][1]), win_tiles(inp["ffn_post_w_in"][1])])
    ffn_wout = np.stack([np.asarray(inp["ffn_pre_w_out"][0], f32), np.asarray(inp["ffn_post_w_out"][0], f32),
                         np.asarray(inp["ffn_pre_w_out"][1], f32), np.asarray(inp["ffn_post_w_out"][1], f32)])
    aw = np.asarray(inp["attn_w_in"][0], f32).reshape(8, 128, 3, 8, 128)
    attn_win = np.ascontiguousarray(aw.transpose(3, 1, 0, 2, 4)).reshape(8, 128, 3072)
    rb = np.asarray(inp["attn_rel_bias"][0], f32)
    ko = np.arange(128)[:, None]
    qo = np.arange(640)[None, :]
    rel = np.clip(qo - ko, -128, 128) + 128
    bt = rb[:, rel]
    bias_tab = np.ascontiguousarray(bt.reshape(8, 2, 128, 640).transpose(0, 2, 1, 3)).reshape(8, 128, 1280)
    lw = np.asarray(inp["lru_w_in"][0], f32).reshape(8, 128, 2, 8, 128)
    lru_win = np.ascontiguousarray(lw.transpose(3, 1, 0, 2, 4)).reshape(8, 128, 2048)
    bd = np.zeros((128, 8, 2, 128), f32)
    for si, key in enumerate(("lru_w_a", "lru_w_x")):
        wb = np.asarray(inp[key][0], f32)
        for c in range(8):
            bd[0:64, c, si, 0:64] = wb[2 * c]
            bd[64:128, c, si, 64:128] = wb[2 * c + 1]
    return {
        "cvec": cvec, "ffn_win": ffn_win, "ffn_wout": ffn_wout, "attn_win": attn_win,
        "attn_wout": np.ascontiguousarray(np.asarray(inp["attn_w_out"][0], f32)),
        "bias_tab": bias_tab, "lru_win": lru_win, "lru_bd": bd.reshape(128, 2048),
        "lru_wout": np.ascontiguousarray(np.asarray(inp["lru_w_out"][0], f32)),
        "ident": np.eye(128, dtype=f32),
    }


_NC_CACHE = {}


def get_nc(n_sub=6):
    if n_sub not in _NC_CACHE:
        _NC_CACHE[n_sub] = Builder(n_sub).build()
    return _NC_CACHE[n_sub]


def kernel(**inputs):
    x = np.asarray(inputs["x"], np.float32)
    shared = prep_shared(inputs)
    in_maps = []
    for b in range(NB):
        m = dict(shared)
        m["xT"] = np.ascontiguousarray(x[b].T)
        in_maps.append(m)
    nc = Builder(6).build()
    res = run_bass_kernel_spmd(nc, in_maps, core_ids=list(range(NB)))
    out = np.stack([np.ascontiguousarray(res.results[b]["outT"].T) for b in range(NB)])
    return out.astype(np.float32)
```

```python
import contextlib
import numpy as np
import concourse.bass as bass
import concourse.mybir as mybir
from concourse.bass_utils import run_bass_kernel_spmd

F32 = mybir.dt.float32
BF16 = mybir.dt.bfloat16
AF = mybir.ActivationFunctionType
ALU = mybir.AluOpType

D = 1024
SEQ = 2048
NB = 8
NCH = 8
NT = 4
TT = 512
DFF = 2816
NF = 22
EPS = 1e-6
NHEAD = 16
GROUPS = [5, 5, 4, 4, 4]
GMAX = 5

ENGS = ("pe", "act", "dve", "pool", "sp")


class T:
    __slots__ = ("name", "last_w", "readers")

    def __init__(self, name=""):
        self.name = name
        self.last_w = None
        self.readers = []


class Op:
    __slots__ = ("eng", "fn", "idx", "deps", "needs_inc", "inc_val", "is_dma", "dma_sem_id", "dma_val")

    def __init__(self, eng, fn, is_dma=False):
        self.eng = eng
        self.fn = fn
        self.idx = -1
        self.deps = []
        self.needs_inc = False
        self.inc_val = 0
        self.is_dma = is_dma
        self.dma_sem_id = None
        self.dma_val = 0


class Prog:
    def __init__(self):
        self.ops = {e: [] for e in ENGS}
        self.pending = {e: [] for e in ENGS}
        self.dma_counts = []

    def new_dma_sem(self):
        self.dma_counts.append(0)
        return len(self.dma_counts) - 1

    def _add(self, op, reads, writes):
        deps = []
        for t in reads:
            if t.last_w is not None:
                deps.append(t.last_w)
        for t in writes:
            if t.last_w is not None:
                deps.append(t.last_w)
            deps.extend(t.readers)
        deps.extend(self.pending[op.eng])
        self.pending[op.eng] = []
        seen = set()
        for d in deps:
            if d is op or id(d) in seen:
                continue
            seen.add(id(d))
            op.deps.append(d)
        for t in reads:
            t.readers.append(op)
        for t in writes:
            t.last_w = op
            t.readers = []
        op.idx = len(self.ops[op.eng])
        self.ops[op.eng].append(op)
        return op

    def op(self, eng, fn, reads=(), writes=(), extra=()):
        self.pending[eng].extend(extra)
        return self._add(Op(eng, fn), list(reads), list(writes))

    def dma(self, eng, fn, sem_id, reads=(), writes=(), extra=()):
        op = Op(eng, fn, is_dma=True)
        op.dma_sem_id = sem_id
        self.dma_counts[sem_id] += 16
        op.dma_val = self.dma_counts[sem_id]
        self.pending[eng].extend(extra)
        return self._add(op, list(reads), list(writes))

    def barrier(self, engs=("pe", "act", "dve")):
        lasts = [self.ops[e][-1] for e in engs if self.ops[e]]
        for e in engs:
            self.pending[e].extend(lasts)

    def emit(self, nc, final_wait_ops=()):
        for e in ENGS:
            for op in self.ops[e]:
                for d in op.deps:
                    if not d.is_dma:
                        d.needs_inc = True
        for e in ENGS:
            c = 0
            for op in self.ops[e]:
                if (not op.is_dma) and op.needs_inc:
                    c += 1
                    op.inc_val = c
        nd = len(self.dma_counts)
        with contextlib.ExitStack() as st:
            esem = {e: st.enter_context(nc.semaphore("s_" + e)) for e in ENGS}
            dsem = [st.enter_context(nc.semaphore("d_%d" % i)) for i in range(nd)]
            block = st.enter_context(nc.Block())

            def run(e, eng):
                waited_e = {x: 0 for x in ENGS}
                waited_d = [0] * nd
                for op in self.ops[e]:
                    waits = []
                    for d in op.deps:
                        if d.is_dma:
                            if waited_d[d.dma_sem_id] < d.dma_val:
                                waits.append((dsem[d.dma_sem_id], d.dma_val))
                                waited_d[d.dma_sem_id] = d.dma_val
                        else:
                            if d.eng == e and e == "pe":
                                continue
                            if waited_e[d.eng] < d.inc_val:
                                waits.append((esem[d.eng], d.inc_val))
                                waited_e[d.eng] = d.inc_val
                    best = {}
                    for sem, val in waits:
                        k = id(sem)
                        if k not in best or best[k][1] < val:
                            best[k] = (sem, val)
                    waits = list(best.values())
                    embed = None
                    if waits and not op.is_dma:
                        embed = waits.pop()
                    for sem, val in waits:
                        eng.wait_ge(sem, val)
                    ins = op.fn(eng)
                    if embed is not None:
                        ins._wait_ge(embed[0], embed[1])
                    if op.is_dma:
                        ins.then_inc(dsem[op.dma_sem_id], 16)
                    elif op.needs_inc:
                        ins.then_inc(esem[e], 1)
                if e == "sp":
                    for op in final_wait_ops:
                        if waited_d[op.dma_sem_id] < op.dma_val:
                            eng.wait_ge(dsem[op.dma_sem_id], op.dma_val)
                            waited_d[op.dma_sem_id] = op.dma_val

            @block.sync
            def _(eng):
                run("sp", eng)

            @block.tensor
            def _(eng):
                run("pe", eng)

            @block.scalar
            def _(eng):
                run("act", eng)

            @block.vector
            def _(eng):
                run("dve", eng)

            @block.gpsimd
            def _(eng):
                run("pool", eng)


CV_GAIN = 0
CV_QG = 48
CV_KG = 49
CV_CW = 50
CV_CB = 82
CV_BA = 90
CV_BX = 98
CV_LAM = 106
NCV = 114


class Builder:
    def __init__(self, n_sub=6):
        self.n_sub = n_sub
        self.nc = bass.Bass("TRN2", target_bir_lowering=False)
        self.P = Prog()
        nc = self.nc
        dt = nc.dram_tensor
        self.xT_d = dt("xT", [D, SEQ], F32, kind="ExternalInput").ap()
        self.cvec_d = dt("cvec", [128, NCV], F32, kind="ExternalInput").ap()
        self.fwin_d = dt("ffn_win", [4, NF, 128, 2048], F32, kind="ExternalInput").ap()
        self.fwout_d = dt("ffn_wout", [4, DFF, D], F32, kind="ExternalInput").ap()
        self.awin_d = dt("attn_win", [8, 128, 3072], F32, kind="ExternalInput").ap()
        self.awout_d = dt("attn_wout", [D, D], F32, kind="ExternalInput").ap()
        self.bias_d = dt("bias_tab", [8, 128, 1280], F32, kind="ExternalInput").ap()
        self.lwin_d = dt("lru_win", [8, 128, 2048], F32, kind="ExternalInput").ap()
        self.lbd_d = dt("lru_bd", [128, 2048], F32, kind="ExternalInput").ap()
        self.lwout_d = dt("lru_wout", [D, D], F32, kind="ExternalInput").ap()
        self.ident_d = dt("ident", [128, 128], F32, kind="ExternalInput").ap()
        self.out_d = dt("outT", [D, SEQ], F32, kind="ExternalOutput").ap()

        total = nc.sbuf_bytes_remaining
        nbytes = (total // 64) * 64 - 64
        self.arena_t = nc.alloc_sbuf_tensor("arena", [128, nbytes // 2], BF16)
        self.arena_bytes = nbytes
        self.off = 0
        self.pall = nc.alloc_psum_tensor("ps", [128, 8 * 512], F32)
        self.psum = [self.pall[:, i * 512:(i + 1) * 512] for i in range(8)]
        self.ps_t = [T("ps%d" % i) for i in range(8)]

        self.xT = self.alloc([128, NCH, SEQ], F32)
        self.x_t = [[T() for _ in range(NT)] for _ in range(NCH)]
        self.hT = self.alloc([128, NCH, SEQ], BF16)
        self.h_t = [[T() for _ in range(NT)] for _ in range(NCH)]
        self.cv = self.alloc([128, NCV], F32)
        self.cv_t = T("cv")
        self.dc = self.alloc([128, 64], F32)
        self.dc_t = T("dc")
        self.ones = self.alloc([128, 128], BF16)
        self.onesA = self.alloc([128, 128], BF16)
        self.onesB = self.alloc([128, 128], BF16)
        self.bones = self.alloc([128, 128], BF16)
        self.ident = self.alloc([128, 128], BF16)
        self.ident_t = T("ident")
        self.const_t = T("consts")
        self.rs = [self.alloc([128, TT], F32) for _ in range(2)]
        self.rs_t = [T(), T()]
        self.NSQ = 2
        self.sq = [self.alloc([128, TT], BF16) for _ in range(self.NSQ)]
        self.sq_t = [T() for _ in range(self.NSQ)]
        self.sq_cnt = 0
        self.rs_cnt = 0
        self.WA = [self.alloc([128, 3072], BF16) for _ in range(3)]
        self.WA_t = [T() for _ in range(3)]
        self.WA_sem = [self.P.new_dma_sem() for _ in range(3)]
        self.NWB = 2 * GMAX
        self.WBflat = self.alloc([128, self.NWB * 1024], BF16)
        self.WBall = self.WBflat.rearrange("p (a b) -> p a b", a=self.NWB)
        self.WB_t = [T() for _ in range(self.NWB)]
        self.WB_sem = [self.P.new_dma_sem() for _ in range(self.NWB)]
        self.WA2_hw_sem = self.P.new_dma_sem()
        self.norm_done = set()
        self.arena_base = self.off
        self.sem_c = self.P.new_dma_sem()
        self.sem_x = [self.P.new_dma_sem() for _ in range(NCH)]
        self.sem_o = self.P.new_dma_sem()
        self.sem_misc = self.P.new_dma_sem()
        self.wa_cnt = 0
        self.out_ops = []

    def alloc(self, shape, dtype):
        esz = 4 if dtype == F32 else 2
        n = 1
        for s in shape[1:]:
            n *= s
        nb = n * esz
        off = self.off
        self.off = (off + nb + 63) // 64 * 64
        assert self.off <= self.arena_bytes, ("SBUF overflow", self.off, self.arena_bytes)
        v = self.arena_t[:, off // 2: off // 2 + nb // 2]
        if dtype == F32:
            v = v.bitcast(F32)
        if len(shape) == 3:
            v = v.rearrange("p (a b) -> p a b", a=shape[1])
        elif len(shape) == 4:
            v = v.rearrange("p (a b c) -> p a b c", a=shape[1], b=shape[2])
        return v

    def phase_begin(self):
        self.off = self.arena_base
        if getattr(self, "pre_norm_lasts", None):
            self.phase_lasts = self.pre_norm_lasts
            self.pre_norm_lasts = None
        else:
            self.phase_lasts = [self.P.ops[e][-1] for e in ("pe", "act", "dve") if self.P.ops[e]]
        for e in ("pe", "act", "dve"):
            self.P.pending[e].extend(self.phase_lasts)

    def mm(self, bank, out_ap, lhsT, rhs, start, stop, reads):
        self.P.op("pe", lambda e: e.matmul(out_ap, lhsT, rhs, start=start, stop=stop),
                  reads=reads, writes=[self.ps_t[bank]])

    def act(self, out, in_, func, reads, writes, scale=None, bias=None):
        kw = {}
        if scale is not None:
            kw["scale"] = scale
        if bias is not None:
            kw["bias"] = bias
        self.P.op("act", lambda e: e.activation(out=out, in_=in_, func=func, **kw), reads=reads, writes=writes)

    def stt(self, out, in0, scalar, in1, op0, op1, reads, writes):
        self.P.op("dve", lambda e: e.scalar_tensor_tensor(out=out, in0=in0, scalar=scalar, in1=in1, op0=op0, op1=op1),
                  reads=reads, writes=writes)

    def tt(self, out, in0, in1, op, reads, writes, eng="dve"):
        self.P.op(eng, lambda e: e.tensor_tensor(out=out, in0=in0, in1=in1, op=op), reads=reads, writes=writes)

    def ts(self, out, in0, s1, s2, op0, op1, reads, writes, eng="dve"):
        if op1 is None:
            self.P.op(eng, lambda e: e.tensor_scalar(out=out, in0=in0, scalar1=s1, scalar2=None, op0=op0),
                      reads=reads, writes=writes)
        else:
            self.P.op(eng, lambda e: e.tensor_scalar(out=out, in0=in0, scalar1=s1, scalar2=s2, op0=op0, op1=op1),
                      reads=reads, writes=writes)

    def cvcol(self, c, n=1):
        return self.cv[:, c:c + n]

    def dccol(self, c, n=1):
        return self.dc[:, c:c + n]

    def init(self):
        P = self.P
        P.dma("sp", lambda e: e.dma_start(out=self.cv, in_=self.cvec_d), self.sem_c, writes=[self.cv_t])
        for t in range(NT):
            grp = []
            for c in range(NCH):
                grp.append(P.dma("sp", lambda e, c=c, t=t: e.dma_start(out=self.xT[:, c, t * TT:(t + 1) * TT],
                                                                        in_=self.xT_d[c * 128:(c + 1) * 128, t * TT:(t + 1) * TT]),
                                 self.sem_x[t], writes=[self.x_t[c][t]]))
            for op in grp:
                op.dma_val = grp[-1].dma_val
        ct = [self.const_t]
        P.op("dve", lambda e: e.memset(self.ones, 1.0), writes=ct)
        P.op("dve", lambda e: e.memset(self.onesA, 0.0), writes=ct)
        P.op("dve", lambda e: e.memset(self.onesA[:, 0:64], 1.0), writes=ct)
        P.op("dve", lambda e: e.memset(self.onesB, 0.0), writes=ct)
        P.op("dve", lambda e: e.memset(self.onesB[:, 64:128], 1.0), writes=ct)
        P.op("dve", lambda e: e.memset(self.bones, 0.0), writes=ct)
        P.op("dve", lambda e: e.memset(self.bones[0:64, 0:64], 1.0), writes=ct)
        P.op("dve", lambda e: e.memset(self.bones[64:128, 64:128], 1.0), writes=ct)
        P.dma("pool", lambda e: e.dma_start(out=self.ident, in_=self.ident_d), self.sem_misc, writes=[self.ident_t])
        rd = [self.cv_t, self.dc_t]
        wr = [self.dc_t]
        lam = self.cvcol(CV_LAM, 8)
        s0 = self.dc[:, 40:48]
        s1 = self.dc[:, 48:56]
        s2 = self.dc[:, 56:64]
        cl = self.dc[:, 8:16]
        self.ts(self.dc[:, 0:1], self.cvcol(CV_KG), 8.0, None, ALU.mult, None, rd, wr)
        self.ts(s1, lam, -1.0, None, ALU.mult, None, rd, wr)
        self.tt(s0, s1, lam, ALU.max, rd, wr)
        self.act(s0, s0, AF.Exp, rd, wr, scale=-1.0)
        self.ts(s1, s0, 1.0, None, ALU.add, None, rd, wr)
        self.act(s2, s1, AF.Ln, rd, wr)
        self.ts(s1, s1, -1.0, 1e-30, ALU.add, ALU.max, rd, wr)
        P.op("dve", lambda e: e.reciprocal(out=s1, in_=s1), reads=rd, writes=wr)
        self.tt(s2, s2, s0, ALU.mult, rd, wr)
        self.tt(s2, s2, s1, ALU.mult, rd, wr)
        self.ts(s0, lam, -1.0, 0.0, ALU.mult, ALU.max, rd, wr)
        self.tt(s2, s2, s0, ALU.add, rd, wr)
        self.ts(cl, s2, -8.0, None, ALU.mult, None, rd, wr)
        self.ts(self.dc[:, 16:24], cl, 0.5, None, ALU.mult, None, rd, wr)
        self.ts(self.dc[:, 24:32], self.cvcol(CV_BA, 8), 0.5, None, ALU.mult, None, rd, wr)
        self.ts(self.dc[:, 32:40], self.cvcol(CV_BX, 8), 0.5, None, ALU.mult, None, rd, wr)

    def rmsnorm_tile(self, n, t):
        if (n, t) in self.norm_done:
            return
        self.norm_done.add((n, t))
        NB_ = 6
        tsl = slice(t * TT, (t + 1) * TT)
        for c in range(NCH):
            k = self.sq_cnt % self.NSQ
            self.sq_cnt += 1
            self.act(self.sq[k], self.xT[:, c, tsl], AF.Square, [self.x_t[c][t]], [self.sq_t[k]])
            self.mm(NB_, self.psum[NB_], self.ones, self.sq[k], c == 0, c == NCH - 1,
                    [self.sq_t[k], self.const_t])
        r = self.rs_cnt % 2
        self.rs_cnt += 1
        self.act(self.rs[r], self.psum[NB_], AF.Ln, [self.ps_t[NB_]], [self.rs_t[r]], scale=1.0 / D, bias=EPS)
        RB = 7
        self.act(self.psum[RB], self.rs[r], AF.Exp, [self.rs_t[r]], [self.ps_t[RB]], scale=-0.5)
        for c in range(NCH):
            self.stt(self.hT[:, c, tsl], self.xT[:, c, tsl], self.cvcol(CV_GAIN + n * 8 + c), self.psum[RB],
                     ALU.mult, ALU.mult, [self.x_t[c][t], self.ps_t[RB], self.cv_t], [self.h_t[c][t]])

    def rmsnorm(self, n):
        for t in range(NT):
            self.rmsnorm_tile(n, t)

    def tail_norm(self, t):
        nn = self.next_norm
        if nn is None:
            return
        if t == NT - 1:
            self.pre_norm_lasts = [self.P.ops[e][-1] for e in ("pe", "act", "dve") if self.P.ops[e]]
        if t >= 1:
            self.rmsnorm_tile(nn, t - 1)
        if t == NT - 1:
            self.rmsnorm_tile(nn, t)

    def ffn(self, fi, n):
        P = self.P
        self.phase_begin()
        actb = [self.alloc([128, GMAX, SEQ], BF16) for _ in range(2)]
        act_t = [[[T() for _ in range(NT)] for _ in range(GMAX)] for _ in range(2)]
        sg = [self.alloc([128, TT], BF16) for _ in range(2)]
        sg_t = [T(), T()]
        groups = []
        j0 = 0
        for g in GROUPS:
            groups.append(list(range(j0, j0 + g)))
            j0 += g
        wa_slot = {}

        def load_win(j):
            s = self.wa_cnt % 3
            self.wa_cnt += 1
            wa_slot[j] = s
            P.dma("pool", lambda e: e.dma_start(out=self.WA[s][:, 0:2048], in_=self.fwin_d[fi, j], max_dma_last_dim=4096),
                  self.WA_sem[s], writes=[self.WA_t[s]])

        def load_wout(gi):
            for jl, j in enumerate(groups[gi]):
                s = (gi % 2) * GMAX + jl
                P.dma("pool", lambda e, s=s, j=j: e.dma_start(out=self.WBall[:, s, :], in_=self.fwout_d[fi, j * 128:(j + 1) * 128, :],
                                                                max_dma_last_dim=4096),
                      self.WB_sem[s], writes=[self.WB_t[s]])

        for j in range(3):
            load_win(j)
        first_phase = (fi == 0)
        if not first_phase:
            load_wout(0)
            load_wout(1)
        self.rmsnorm(n)
        cnt = [0, 0]

        def UG(gi, jl, j):
            s = wa_slot[j]
            w = self.WA[s][:, 0:2048].rearrange("p (k m) -> p k m", k=8)
            for t in range(NT):
                b = cnt[0] % 2
                cnt[0] += 1
                tsl = slice(t * TT, (t + 1) * TT)
                for kc in range(NCH):
                    self.mm(b, self.psum[b], w[:, kc, 0:128], self.hT[:, kc, tsl], kc == 0, kc == NCH - 1,
                            [self.WA_t[s], self.h_t[kc][t]])
                for kc in range(NCH):
                    self.mm(2 + b, self.psum[2 + b], w[:, kc, 128:256], self.hT[:, kc, tsl], kc == 0, kc == NCH - 1,
                            [self.WA_t[s], self.h_t[kc][t]])
                self.act(sg[b], self.psum[b], AF.Silu, [self.ps_t[b]], [sg_t[b]])
                self.tt(actb[gi % 2][:, jl, tsl], sg[b], self.psum[2 + b], ALU.mult,
                        [sg_t[b], self.ps_t[2 + b]], [act_t[gi % 2][jl][t]])
            if j + 3 < NF:
                load_win(j + 3)
            if first_phase and j < 2:
                load_wout(j)

        def OUT(gi, is_last):
            grp = groups[gi]
            for t in range(NT):
                tsl = slice(t * TT, (t + 1) * TT)
                for c in range(NCH):
                    b = 4 + cnt[1] % 2
                    cnt[1] += 1
                    for jl, j in enumerate(grp):
                        s = (gi % 2) * GMAX + jl
                        self.mm(b, self.psum[b], self.WBall[:, s, c * 128:(c + 1) * 128], actb[gi % 2][:, jl, tsl],
                                jl == 0, jl == len(grp) - 1, [self.WB_t[s], act_t[gi % 2][jl][t]])
                    self.stt(self.xT[:, c, tsl], self.psum[b], 0.5, self.xT[:, c, tsl], ALU.mult, ALU.add,
                             [self.ps_t[b], self.x_t[c][t]], [self.x_t[c][t]])
                    if is_last and self.final_phase:
                        self.store_tile(c, t)
                if is_last:
                    self.tail_norm(t)
            if gi + 2 < len(groups):
                load_wout(gi + 2)

        for gi, grp in enumerate(groups):
            for jl, j in enumerate(grp):
                UG(gi, jl, j)
                if jl == 0 and gi > 0:
                    OUT(gi - 1, False)
        OUT(len(groups) - 1, True)

    def attention(self, n):
        P = self.P
        self.phase_begin()
        attnT = self.alloc([128, NCH, SEQ], BF16)
        at_t = [[T() for _ in range(NT)] for _ in range(NCH)]
        qT = self.alloc([128, SEQ], BF16)
        q_t = [T() for _ in range(NT)]
        kTz = [self.alloc([128, SEQ], BF16) for _ in range(2)]
        k_t = [[T() for _ in range(NT)] for _ in range(2)]
        vz = [self.alloc([128, 16, 128], BF16) for _ in range(2)]
        v_t = [[T() for _ in range(16)] for _ in range(2)]
        expB = self.alloc([128, 2, 640], BF16)
        eb_t = T("expB")
        NE = 3
        E = [self.alloc([128, 2, TT], BF16) for _ in range(NE)]
        E_t = [T() for _ in range(NE)]
        Osb = [self.alloc([128, TT], F32) for _ in range(2)]
        Osb_t = [T(), T()]
        P.op("pool", lambda e: e.memset(kTz[0], 0.0), writes=k_t[0], extra=self.phase_lasts)
        P.op("pool", lambda e: e.memset(kTz[1], 0.0), writes=k_t[1])
        P.op("pool", lambda e: e.memset(vz[0], 0.0), writes=v_t[0])
        P.op("pool", lambda e: e.memset(vz[1], 0.0), writes=v_t[1])

        def load_pair(i):
            s = i % 2
            P.dma("pool", lambda e: e.dma_start(out=self.WA[s], in_=self.awin_d[i], max_dma_last_dim=4096),
                  self.WA_sem[s], writes=[self.WA_t[s]])

        def load_bias(i):
            P.dma("sp", lambda e: e.dma_start(out=self.WA[2][:, 0:2560].bitcast(F32), in_=self.bias_d[i]),
                  self.WA2_hw_sem, writes=[self.WA_t[2]])

        load_pair(0)
        load_bias(0)
        load_pair(1)
        self.rmsnorm(n)
        pc = [0]
        cc = [0, 0]
        qg = self.cvcol(CV_QG)
        kg8 = self.dccol(0)
        for i in range(8):
            s = i % 2
            w = self.WA[s].rearrange("p (k m) -> p k m", k=8)
            wt = self.WA_t[s]
            bsrc = self.WA[2][:, 0:2560].bitcast(F32).rearrange("p (h q) -> p h q", h=2)
            self.act(expB, bsrc, AF.Copy, [self.WA_t[2]], [eb_t])
            P.op("dve", lambda e: e.memset(expB[64:128, :, 0:64], -30000.0), reads=[], writes=[eb_t])
            P.op("dve", lambda e: e.memset(expB[0:64, :, 576:640], -30000.0), reads=[], writes=[eb_t])
            if i + 1 < 8:
                load_bias(i + 1)
            pitems = [(which, t) for which in range(2) for t in range(NT)]
            pbank = {}

            def pstage1(m):
                which, t = pitems[m]
                tsl = slice(t * TT, (t + 1) * TT)
                b = pc[0] % 3
                pc[0] += 1
                pbank[m] = b
                for kc in range(NCH):
                    self.mm(b, self.psum[b], w[:, kc, which * 128:(which + 1) * 128], self.hT[:, kc, tsl],
                            kc == 0, kc == NCH - 1, [wt, self.h_t[kc][t]])
                k = self.sq_cnt % self.NSQ
                self.sq_cnt += 1
                pbank[(m, "sq")] = k
                self.act(self.sq[k], self.psum[b], AF.Square, [self.ps_t[b]], [self.sq_t[k]])

            def pstage2(m):
                which, t = pitems[m]
                tsl = slice(t * TT, (t + 1) * TT)
                b = pbank[m]
                k = pbank[(m, "sq")]
                self.mm(3, self.psum[3], self.bones, self.sq[k], True, True, [self.sq_t[k], self.const_t])
                r = self.rs_cnt % 2
                self.rs_cnt += 1
                self.act(self.rs[r], self.psum[3], AF.Ln, [self.ps_t[3]], [self.rs_t[r]], scale=1.0, bias=64.0 * EPS)
                self.act(self.rs[r], self.rs[r], AF.Exp, [self.rs_t[r]], [self.rs_t[r]], scale=-0.5)
                if which == 0:
                    self.stt(qT[:, tsl], self.psum[b], qg, self.rs[r], ALU.mult, ALU.mult,
                             [self.ps_t[b], self.rs_t[r], self.cv_t], [q_t[t]])
                else:
                    for hh in range(2):
                        ps_ = slice(hh * 64, (hh + 1) * 64)
                        self.stt(kTz[hh][ps_, tsl], self.psum[b][ps_, :], kg8[ps_, :], self.rs[r][ps_, :], ALU.mult, ALU.mult,
                                 [self.ps_t[b], self.rs_t[r], self.dc_t], [k_t[hh][t]])

            for m in range(len(pitems) + 1):
                if m < len(pitems):
                    pstage1(m)
                if m >= 1:
                    pstage2(m - 1)
            for kt in range(16):
                t = kt // 4
                b = pc[0] % 3
                pc[0] += 1
                ksl = slice(kt * 128, (kt + 1) * 128)
                for kc in range(NCH):
                    self.mm(b, self.psum[b][:, 0:128], self.hT[:, kc, ksl], w[:, kc, 256:384], kc == 0, kc == NCH - 1,
                            [wt, self.h_t[kc][t]])
                for hh in range(2):
                    cs = slice(hh * 64, (hh + 1) * 64)
                    P.op("dve", lambda e, hh=hh, kt=kt, cs=cs, b=b: e.tensor_copy(out=vz[hh][:, kt, cs], in_=self.psum[b][:, cs]),
                         reads=[self.ps_t[b]], writes=[v_t[hh][kt]])
            if i + 2 < 8:
                load_pair(i + 2)
            seq = []
            for t in range(NT):
                kts = [kt for kt in (4 * t, 4 * t - 1, 4 * t + 1, 4 * t - 2, 4 * t + 2, 4 * t - 3, 4 * t + 3, 4 * t - 4)
                       if 0 <= kt < 16]
                for idx, kt in enumerate(kts):
                    seq.append((t, kt, idx == 0, idx == len(kts) - 1))
            st = {}

            def cstage1(k):
                t, kt, first, last = seq[k]
                a = max(TT * t, 128 * kt)
                bnd = min(TT * t + TT, 128 * kt + 640)
                nq = bnd - a
                sb = 2 * (cc[1] % 3)
                r = cc[1] % NE
                cc[1] += 1
                st[k] = (a, bnd, nq, r)
                for hh in range(2):
                    self.mm(sb + hh, self.psum[sb + hh][:, 0:nq], kTz[hh][:, kt * 128:(kt + 1) * 128], qT[:, a:bnd], True, False,
                            [k_t[hh][kt // 4], q_t[t]])
                    self.mm(sb + hh, self.psum[sb + hh][:, 0:nq], self.ident, expB[:, hh, a - 128 * kt: bnd - 128 * kt], False, True,
                            [self.ident_t, eb_t])
                sview = self.pall[:, sb * 512:(sb + 2) * 512].rearrange("p (h q) -> p h q", h=2)[:, :, 0:nq]
                self.act(E[r][:, :, 0:nq], sview, AF.Exp, [self.ps_t[sb], self.ps_t[sb + 1]], [E_t[r]])

            def cstage2(k):
                t, kt, first, last = seq[k]
                a, bnd, nq, r = st[k]
                ob = 6
                db = 7
                osl = slice(a - TT * t, bnd - TT * t)
                for hh in range(2):
                    self.mm(ob, self.psum[ob][:, osl], vz[hh][:, kt, :], E[r][:, hh, 0:nq], first and hh == 0, last and hh == 1,
                            [v_t[hh][kt], E_t[r]])
                    self.mm(db, self.psum[db][:, osl], self.onesA if hh == 0 else self.onesB,
                            E[r][:, hh, 0:nq], first and hh == 0, last and hh == 1, [self.const_t, E_t[r]])
                if last:
                    rr = self.rs_cnt % 2
                    self.rs_cnt += 1
                    oo = cc[0] % 2
                    cc[0] += 1
                    self.act(self.rs[rr], self.psum[db], AF.Ln, [self.ps_t[db]], [self.rs_t[rr]])
                    P.op("dve", lambda e: e.tensor_copy(out=Osb[oo], in_=self.psum[ob]), reads=[self.ps_t[ob]], writes=[Osb_t[oo]])
                    self.act(self.rs[rr], self.rs[rr], AF.Exp, [self.rs_t[rr]], [self.rs_t[rr]], scale=-1.0)
                    self.tt(attnT[:, i, t * TT:(t + 1) * TT], Osb[oo], self.rs[rr], ALU.mult,
                            [Osb_t[oo], self.rs_t[rr]], [at_t[i][t]])

            LA = 2
            for k in range(len(seq) + LA):
                if k < len(seq):
                    cstage1(k)
                if k >= LA:
                    cstage2(k - LA)
        self.out_proj(self.awout_d, attnT, at_t)

    def out_proj(self, w_d, yT, y_t, extra=()):
        P = self.P
        for i in range(8):
            P.dma("pool", lambda e, i=i: e.dma_start(out=self.WBall[:, i, :], in_=w_d[i * 128:(i + 1) * 128, :], max_dma_last_dim=4096),
                  self.WB_sem[i], writes=[self.WB_t[i]], extra=extra)
        cnt = 0
        for t in range(NT):
            tsl = slice(t * TT, (t + 1) * TT)
            for c in range(NCH):
                b = cnt % 2
                cnt += 1
                for i in range(8):
                    self.mm(b, self.psum[b], self.WBall[:, i, c * 128:(c + 1) * 128], yT[:, i, tsl], i == 0, i == 7,
                            [self.WB_t[i], y_t[i][t]])
                self.tt(self.xT[:, c, tsl], self.psum[b], self.xT[:, c, tsl], ALU.add,
                        [self.ps_t[b], self.x_t[c][t]], [self.x_t[c][t]])
            self.tail_norm(t)

    def lru(self, n):
        P = self.P
        self.phase_begin()
        H = 2 * TT
        yT = self.alloc([128, NCH, SEQ], BF16)
        y_t = [[T() for _ in range(NT)] for _ in range(NCH)]
        bd = [self.alloc([128, 2, 128], F32) for _ in range(2)]
        bd_t = [T(), T()]
        bd_sem = [P.new_dma_sem(), P.new_dma_sem()]
        halo = [self.alloc([128, 4], F32) for _ in range(2)]
        halo_t = [T(), T()]
        carry = self.alloc([128, 4], F32)
        carry_t = T()
        names = ("xc", "tha", "thx", "Ab", "gw")
        bufs = [{}, {}]
        tts = [{}, {}]
        for j, nm in enumerate(names):
            bufs[0][nm] = self.alloc([128, H], F32)
            bufs[1][nm] = self.WBflat[:, j * 2 * H:(j + 1) * 2 * H].bitcast(F32)
            tts[0][nm] = T()
            tts[1][nm] = T()
        xc3 = [bufs[0]["xc"], bufs[1]["xc"], self.alloc([128, H], F32)]
        xc3_t = [tts[0]["xc"], tts[1]["xc"], T()]
        lbd = self.lbd_d.rearrange("p (c s m) -> p c s m", c=8, s=2)
        xb2 = self.pall[:, 0:2 * TT]
        g2 = self.pall[:, 2 * TT:4 * TT]
        ra2 = self.pall[:, 4 * TT:6 * TT]
        rx2 = self.pall[:, 6 * TT:8 * TT]
        XB, GB, RA, RX = [self.ps_t[0], self.ps_t[1]], [self.ps_t[2], self.ps_t[3]], [self.ps_t[4], self.ps_t[5]], [self.ps_t[6], self.ps_t[7]]

        def load_bd(c, extra=()):
            b = c % 2
            P.dma("sp", lambda e: e.dma_start(out=bd[b], in_=lbd[:, c, :, :]), bd_sem[b], writes=[bd_t[b]], extra=extra)

        def load_w(c):
            s = self.wa_cnt % 3
            self.wa_cnt += 1
            P.dma("pool", lambda e: e.dma_start(out=self.WA[s][:, 0:2048], in_=self.lwin_d[c], max_dma_last_dim=4096),
                  self.WA_sem[s], writes=[self.WA_t[s]])
            return s

        slots = {}
        for c in range(3):
            slots[c] = load_w(c)
        load_bd(0, extra=self.phase_lasts)
        load_bd(1, extra=self.phase_lasts)
        self.rmsnorm(n)
        units = [(c, m) for c in range(NCH) for m in range(2)]
        nu = len(units)
        xbv = [self.pall[:, 0:2 * TT], self.pall[:, 2 * TT:4 * TT]]
        XBT = [[self.ps_t[0], self.ps_t[1]], [self.ps_t[2], self.ps_t[3]]]
        g2 = self.pall[:, 4 * TT:6 * TT]
        GB = [self.ps_t[4], self.ps_t[5]]

        def XB(k):
            c, m = units[k]
            s = slots[c]
            w = self.WA[s][:, 0:2048].rearrange("p (k m) -> p k m", k=8)
            wt = self.WA_t[s]
            pb = 2 * (k % 2)
            for hf in range(2):
                t = 2 * m + hf
                tsl = slice(t * TT, (t + 1) * TT)
                for kc in range(NCH):
                    self.mm(pb + hf, self.psum[pb + hf], w[:, kc, 0:128], self.hT[:, kc, tsl], kc == 0, kc == NCH - 1,
                            [wt, self.h_t[kc][t]])

        def CONV(k):
            c, m = units[k]
            u = k % 2
            pu = (k - 1) % 2
            xb2 = xbv[k % 2]
            XBk = XBT[k % 2]
            cw = [self.cvcol(CV_CW + j * 8 + c) for j in range(4)]
            cb = self.cvcol(CV_CB + c)
            P.op("dve", lambda e: e.tensor_copy(out=halo[u][:, 0:3], in_=xb2[:, H - 3:H]), reads=XBk, writes=[halo_t[u]])
            xc = xc3[k % 3]
            xct = xc3_t[k % 3]
            self.ts(xc, xb2, cw[3], cb, ALU.mult, ALU.add, XBk + [self.cv_t], [xct])
            for j in (2, 1, 0):
                sh = 3 - j
                self.stt(xc[:, sh:H], xb2[:, 0:H - sh], cw[j], xc[:, sh:H], ALU.mult, ALU.add,
                         XBk + [self.cv_t, xct], [xct])
                if m > 0:
                    self.stt(xc[:, 0:sh], halo[pu][:, 3 - sh:3], cw[j], xc[:, 0:sh], ALU.mult, ALU.add,
                             [halo_t[pu], self.cv_t, xct], [xct])

        def G(k):
            c, m = units[k]
            u = k % 2
            B_, Tt = bufs[u], tts[u]
            xc = xc3[k % 3]
            xct = xc3_t[k % 3]
            b = c % 2
            hba = self.dccol(24 + c)
            hbx = self.dccol(32 + c)
            for hf in range(2):
                hs = slice(hf * TT, (hf + 1) * TT)
                self.mm(6, self.psum[6], bd[b][:, 0, :], xc[:, hs], True, True, [bd_t[b], xct])
                self.mm(7, self.psum[7], bd[b][:, 1, :], xc[:, hs], True, True, [bd_t[b], xct])
                self.act(B_["tha"][:, hs], self.psum[6], AF.Tanh, [self.ps_t[6], self.dc_t], [Tt["tha"]], scale=0.5, bias=hba)
                self.act(B_["thx"][:, hs], self.psum[7], AF.Tanh, [self.ps_t[7], self.dc_t], [Tt["thx"]], scale=0.5, bias=hbx)
            if m == 1 and c + 2 < NCH:
                load_bd(c + 2)

        def A2(k):
            c, m = units[k]
            s = slots[c]
            w = self.WA[s][:, 0:2048].rearrange("p (k m) -> p k m", k=8)
            wt = self.WA_t[s]
            for hf in range(2):
                t = 2 * m + hf
                tsl = slice(t * TT, (t + 1) * TT)
                for kc in range(NCH):
                    self.mm(4 + hf, self.psum[4 + hf], w[:, kc, 128:256], self.hT[:, kc, tsl], kc == 0, kc == NCH - 1,
                            [wt, self.h_t[kc][t]])
        def B_rest(k):
            c, m = units[k]
            u = k % 2
            B_, Tt = bufs[u], tts[u]
            cl = self.dccol(8 + c)
            hcl = self.dccol(16 + c)
            tha, Ab, gw = B_["tha"], B_["Ab"], B_["gw"]
            self.act(Ab, tha, AF.Exp, [Tt["tha"], self.dc_t], [Tt["Ab"]], scale=hcl, bias=hcl)
            self.act(tha, tha, AF.Exp, [Tt["tha"], self.dc_t], [Tt["tha"]], scale=cl, bias=cl)
            self.ts(tha, tha, 1.0, None, ALU.min, None, [Tt["tha"]], [Tt["tha"]])
            self.act(gw, g2, AF.Square, GB, [Tt["gw"]], scale=0.21145921)
            self.stt(gw, gw, 1.0, g2, ALU.add, ALU.mult, [Tt["gw"]] + GB, [Tt["gw"]])
            self.act(gw, gw, AF.Tanh, [Tt["gw"]], [Tt["gw"]], scale=0.7978845608)
            self.stt(gw, gw, 1.0, g2, ALU.add, ALU.mult, [Tt["gw"]] + GB, [Tt["gw"]])
            self.act(tha, tha, AF.Sqrt, [Tt["tha"]], [Tt["tha"]], scale=-0.0625, bias=0.0625)

        def C(k):
            c, m = units[k]
            u = k % 2
            B_, Tt = bufs[u], tts[u]
            tha, thx, Ab, gw = B_["tha"], B_["thx"], B_["Ab"], B_["gw"]
            xc = xc3[k % 3]
            xct = xc3_t[k % 3]
            self.ts(thx, thx, 1.0, 1.0, ALU.add, ALU.mult, [Tt["thx"]], [Tt["thx"]], eng="pool")
            self.tt(thx, thx, xc, ALU.mult, [Tt["thx"], xct], [Tt["thx"]], eng="pool")
            self.tt(thx, thx, tha, ALU.mult, [Tt["thx"], Tt["tha"]], [Tt["thx"]], eng="pool")
            if m == 0:
                P.op("dve", lambda e: e.tensor_tensor_scan(out=tha, data0=Ab, data1=thx, initial=0.0,
                                                           op0=ALU.mult, op1=ALU.add),
                     reads=[Tt["Ab"], Tt["thx"]], writes=[Tt["tha"]])
                P.op("dve", lambda e: e.tensor_copy(out=carry[:, 0:1], in_=tha[:, H - 1:H]), reads=[Tt["tha"]], writes=[carry_t])
            else:
                P.op("dve", lambda e: e.tensor_tensor_scan(out=tha, data0=Ab, data1=thx,
                                                           initial=carry[:, 0:1], op0=ALU.mult, op1=ALU.add),
                     reads=[Tt["Ab"], Tt["thx"], carry_t], writes=[Tt["tha"]])
            self.tt(yT[:, c, m * H:(m + 1) * H], gw, tha, ALU.mult, [Tt["gw"], Tt["tha"]],
                    [y_t[c][2 * m], y_t[c][2 * m + 1]], eng="pool")

        XB(0)
        CONV(0)
        G(0)
        A2(0)
        XB(1)
        CONV(1)
        XB(2)
        for k in range(nu):
            B_rest(k)
            if k + 1 < nu:
                G(k + 1)
                A2(k + 1)
            if k + 2 < nu:
                CONV(k + 2)
            if k + 3 < nu:
                XB(k + 3)
            C(k)
            if k + 1 < nu:
                c1, m1 = units[k + 1]
                if m1 == 1 and c1 + 3 < NCH:
                    slots[c1 + 3] = load_w(c1 + 3)
        lasts = [P.ops[e][-1] for e in ("pe", "act", "dve", "pool")]
        self.out_proj(self.lwout_d, yT, y_t, extra=lasts)

    def store_tile(self, c, t):
        tsl = slice(t * TT, (t + 1) * TT)
        op = self.P.dma("sp", lambda e: e.dma_start(out=self.out_d[c * 128:(c + 1) * 128, tsl], in_=self.xT[:, c, tsl]),
                        self.sem_o, reads=[self.x_t[c][t]])
        self.out_ops.append(op)
        self.stored.add((c, t))

    def finish(self):
        P = self.P
        for t in range(NT):
            for c in range(NCH):
                if (c, t) not in self.stored:
                    self.store_tile(c, t)
        for op in self.out_ops:
            op.dma_val = self.out_ops[-1].dma_val
        P.emit(self.nc, final_wait_ops=self.out_ops)

    def build(self):
        self.init()
        subs = [lambda: self.ffn(0, 0), lambda: self.attention(1), lambda: self.ffn(1, 2),
                lambda: self.ffn(2, 3), lambda: self.lru(4), lambda: self.ffn(3, 5)]
        self.stored = set()
        for k in range(self.n_sub):
            self.next_norm = (k + 1) if k + 1 < self.n_sub else None
            self.final_phase = (k == self.n_sub - 1)
            subs[k]()
        self.finish()
        return self.nc


def prep_shared(inp):
    f32 = np.float32
    cvec = np.zeros((128, NCV), f32)
    norms = [inp["norm_ffn_pre"][0], inp["norm_mix"][0], inp["norm_ffn_post"][0],
             inp["norm_ffn_pre"][1], inp["norm_mix"][1], inp["norm_ffn_post"][1]]
    for n, g in enumerate(norms):
        cvec[:, CV_GAIN + n * 8: CV_GAIN + n * 8 + 8] = np.asarray(g, f32).reshape(8, 128).T
    cvec[:, CV_QG] = np.tile(np.asarray(inp["attn_q_gain"][0], f32), 2)
    cvec[:, CV_KG] = np.tile(np.asarray(inp["attn_k_gain"][0], f32), 2)
    cw = np.asarray(inp["lru_conv_w"][0], f32)
    for j in range(4):
        cvec[:, CV_CW + j * 8: CV_CW + j * 8 + 8] = cw[j].reshape(8, 128).T
    cvec[:, CV_CB:CV_CB + 8] = np.asarray(inp["lru_conv_b"][0], f32).reshape(8, 128).T
    cvec[:, CV_BA:CV_BA + 8] = np.asarray(inp["lru_b_a"][0], f32).reshape(8, 128).T
    cvec[:, CV_BX:CV_BX + 8] = np.asarray(inp["lru_b_x"][0], f32).reshape(8, 128).T
    cvec[:, CV_LAM:CV_LAM + 8] = np.asarray(inp["lru_lambda"][0], f32).reshape(8, 128).T

    def win_tiles(w):
        w = np.asarray(w, f32).reshape(8, 128, 2, NF, 128)
        return np.ascontiguousarray(w.transpose(3, 1, 0, 2, 4)).reshape(NF, 128, 2048)

    ffn_win = np.stack([win_tiles(inp["ffn_pre_w_in"][0]), win_tiles(inp["ffn_post_w_in"][0]),
                        win_tiles(inp["ffn_pre_w_in"][1]), win_tiles(inp["ffn_post_w_in"][1])])
    ffn_wout = np.stack([np.asarray(inp["ffn_pre_w_out"][0], f32), np.asarray(inp["ffn_post_w_out"][0], f32),
                         np.asarray(inp["ffn_pre_w_out"][1], f32), np.asarray(inp["ffn_post_w_out"][1], f32)])
    aw = np.asarray(inp["attn_w_in"][0], f32).reshape(8, 128, 3, 8, 128)
    attn_win = np.ascontiguousarray(aw.transpose(3, 1, 0, 2, 4)).reshape(8, 128, 3072)
    rb = np.asarray(inp["attn_rel_bias"][0], f32)
    ko = np.arange(128)[:, None]
    qo = np.arange(640)[None, :]
    rel = np.clip(qo - ko, -128, 128) + 128
    bt = rb[:, rel]
    bias_tab = np.ascontiguousarray(bt.reshape(8, 2, 128, 640).transpose(0, 2, 1, 3)).reshape(8, 128, 1280)
    lw = np.asarray(inp["lru_w_in"][0], f32).reshape(8, 128, 2, 8, 128)
    lru_win = np.ascontiguousarray(lw.transpose(3, 1, 0, 2, 4)).reshape(8, 128, 2048)
    bd = np.zeros((128, 8, 2, 128), f32)
    for si, key in enumerate(("lru_w_a", "lru_w_x")):
        wb = np.asarray(inp[key][0], f32)
        for c in range(8):
            bd[0:64, c, si, 0:64] = wb[2 * c]
            bd[64:128, c, si, 64:128] = wb[2 * c + 1]
    return {
        "cvec": cvec, "ffn_win": ffn_win, "ffn_wout": ffn_wout, "attn_win": attn_win,
        "attn_wout": np.ascontiguousarray(np.asarray(inp["attn_w_out"][0], f32)),
        "bias_tab": bias_tab, "lru_win": lru_win, "lru_bd": bd.reshape(128, 2048),
        "lru_wout": np.ascontiguousarray(np.asarray(inp["lru_w_out"][0], f32)),
        "ident": np.eye(128, dtype=f32),
    }


_NC_CACHE = {}


def get_nc(n_sub=6):
    if n_sub not in _NC_CACHE:
        _NC_CACHE[n_sub] = Builder(n_sub).build()
    return _NC_CACHE[n_sub]


def kernel(**inputs):
    x = np.asarray(inputs["x"], np.float32)
    shared = prep_shared(inputs)
    in_maps = []
    for b in range(NB):
        m = dict(shared)
        m["xT"] = np.ascontiguousarray(x[b].T)
        in_maps.append(m)
    nc = Builder(6).build()
    res = run_bass_kernel_spmd(nc, in_maps, core_ids=list(range(NB)))
    out = np.stack([np.ascontiguousarray(res.results[b]["outT"].T) for b in range(NB)])
    return out.astype(np.float32)
```

```python
import contextlib
import numpy as np
import concourse.bass as bass
import concourse.mybir as mybir
from concourse.bass_utils import run_bass_kernel_spmd

F32 = mybir.dt.float32
BF16 = mybir.dt.bfloat16
AF = mybir.ActivationFunctionType
ALU = mybir.AluOpType

D = 1024
SEQ = 2048
NB = 8
NCH = 8
NT = 4
TT = 512
DFF = 2816
NF = 22
EPS = 1e-6
NHEAD = 16
GROUPS = [5, 5, 4, 4, 4]
GMAX = 5

ENGS = ("pe", "act", "dve", "pool", "sp")


class T:
    __slots__ = ("name", "last_w", "readers")

    def __init__(self, name=""):
        self.name = name
        self.last_w = None
        self.readers = []


class Op:
    __slots__ = ("eng", "fn", "idx", "deps", "needs_inc", "inc_val", "is_dma", "dma_sem_id", "dma_val")

    def __init__(self, eng, fn, is_dma=False):
        self.eng = eng
        self.fn = fn
        self.idx = -1
        self.deps = []
        self.needs_inc = False
        self.inc_val = 0
        self.is_dma = is_dma
        self.dma_sem_id = None
        self.dma_val = 0


class Prog:
    def __init__(self):
        self.ops = {e: [] for e in ENGS}
        self.pending = {e: [] for e in ENGS}
        self.dma_counts = []

    def new_dma_sem(self):
        self.dma_counts.append(0)
        return len(self.dma_counts) - 1

    def _add(self, op, reads, writes):
        deps = []
        for t in reads:
            if t.last_w is not None:
                deps.append(t.last_w)
        for t in writes:
            if t.last_w is not None:
                deps.append(t.last_w)
            deps.extend(t.readers)
        deps.extend(self.pending[op.eng])
        self.pending[op.eng] = []
        seen = set()
        for d in deps:
            if d is op or id(d) in seen:
                continue
            seen.add(id(d))
            op.deps.append(d)
        for t in reads:
            t.readers.append(op)
        for t in writes:
            t.last_w = op
            t.readers = []
        op.idx = len(self.ops[op.eng])
        self.ops[op.eng].append(op)
        return op

    def op(self, eng, fn, reads=(), writes=(), extra=()):
        self.pending[eng].extend(extra)
        return self._add(Op(eng, fn), list(reads), list(writes))

    def dma(self, eng, fn, sem_id, reads=(), writes=(), extra=()):
        op = Op(eng, fn, is_dma=True)
        op.dma_sem_id = sem_id
        self.dma_counts[sem_id] += 16
        op.dma_val = self.dma_counts[sem_id]
        self.pending[eng].extend(extra)
        return self._add(op, list(reads), list(writes))

    def barrier(self, engs=("pe", "act", "dve")):
        lasts = [self.ops[e][-1] for e in engs if self.ops[e]]
        for e in engs:
            self.pending[e].extend(lasts)

    def emit(self, nc, final_wait_ops=()):
        for e in ENGS:
            for op in self.ops[e]:
                for d in op.deps:
                    if not d.is_dma:
                        d.needs_inc = True
        for e in ENGS:
            c = 0
            for op in self.ops[e]:
                if (not op.is_dma) and op.needs_inc:
                    c += 1
                    op.inc_val = c
        nd = len(self.dma_counts)
        with contextlib.ExitStack() as st:
            esem = {e: st.enter_context(nc.semaphore("s_" + e)) for e in ENGS}
            dsem = [st.enter_context(nc.semaphore("d_%d" % i)) for i in range(nd)]
            block = st.enter_context(nc.Block())

            def run(e, eng):
                waited_e = {x: 0 for x in ENGS}
                waited_d = [0] * nd
                for op in self.ops[e]:
                    waits = []
                    for d in op.deps:
                        if d.is_dma:
                            if waited_d[d.dma_sem_id] < d.dma_val:
                                waits.append((dsem[d.dma_sem_id], d.dma_val))
                                waited_d[d.dma_sem_id] = d.dma_val
                        else:
                            if d.eng == e and e == "pe":
                                continue
                            if waited_e[d.eng] < d.inc_val:
                                waits.append((esem[d.eng], d.inc_val))
                                waited_e[d.eng] = d.inc_val
                    best = {}
                    for sem, val in waits:
                        k = id(sem)
                        if k not in best or best[k][1] < val:
                            best[k] = (sem, val)
                    waits = list(best.values())
                    embed = None
                    if waits and not op.is_dma:
                        embed = waits.pop()
                    for sem, val in waits:
                        eng.wait_ge(sem, val)
                    ins = op.fn(eng)
                    if embed is not None:
                        ins._wait_ge(embed[0], embed[1])
                    if op.is_dma:
                        ins.then_inc(dsem[op.dma_sem_id], 16)
                    elif op.needs_inc:
                        ins.then_inc(esem[e], 1)
                if e == "sp":
                    for op in final_wait_ops:
                        if waited_d[op.dma_sem_id] < op.dma_val:
                            eng.wait_ge(dsem[op.dma_sem_id], op.dma_val)
                            waited_d[op.dma_sem_id] = op.dma_val

            @block.sync
            def _(eng):
                run("sp", eng)

            @block.tensor
            def _(eng):
                run("pe", eng)

            @block.scalar
            def _(eng):
                run("act", eng)

            @block.vector
            def _(eng):
                run("dve", eng)

            @block.gpsimd
            def _(eng):
                run("pool", eng)


CV_GAIN = 0
CV_QG = 48
CV_KG = 49
CV_CW = 50
CV_CB = 82
CV_BA = 90
CV_BX = 98
CV_LAM = 106
NCV = 114


class Builder:
    def __init__(self, n_sub=6):
        self.n_sub = n_sub
        self.nc = bass.Bass("TRN2", target_bir_lowering=False)
        self.P = Prog()
        nc = self.nc
        dt = nc.dram_tensor
        self.xT_d = dt("xT", [D, SEQ], F32, kind="ExternalInput").ap()
        self.cvec_d = dt("cvec", [128, NCV], F32, kind="ExternalInput").ap()
        self.fwin_d = dt("ffn_win", [4, NF, 128, 2048], F32, kind="ExternalInput").ap()
        self.fwout_d = dt("ffn_wout", [4, DFF, D], F32, kind="ExternalInput").ap()
        self.awin_d = dt("attn_win", [8, 128, 3072], F32, kind="ExternalInput").ap()
        self.awout_d = dt("attn_wout", [D, D], F32, kind="ExternalInput").ap()
        self.bias_d = dt("bias_tab", [8, 128, 1280], F32, kind="ExternalInput").ap()
        self.lwin_d = dt("lru_win", [8, 128, 2048], F32, kind="ExternalInput").ap()
        self.lbd_d = dt("lru_bd", [128, 2048], F32, kind="ExternalInput").ap()
        self.lwout_d = dt("lru_wout", [D, D], F32, kind="ExternalInput").ap()
        self.ident_d = dt("ident", [128, 128], F32, kind="ExternalInput").ap()
        self.out_d = dt("outT", [D, SEQ], F32, kind="ExternalOutput").ap()

        total = nc.sbuf_bytes_remaining
        nbytes = (total // 64) * 64 - 64
        self.arena_t = nc.alloc_sbuf_tensor("arena", [128, nbytes // 2], BF16)
        self.arena_bytes = nbytes
        self.off = 0
        self.pall = nc.alloc_psum_tensor("ps", [128, 8 * 512], F32)
        self.psum = [self.pall[:, i * 512:(i + 1) * 512] for i in range(8)]
        self.ps_t = [T("ps%d" % i) for i in range(8)]

        self.xT = self.alloc([128, NCH, SEQ], F32)
        self.x_t = [[T() for _ in range(NT)] for _ in range(NCH)]
        self.hT = self.alloc([128, NCH, SEQ], BF16)
        self.h_t = [[T() for _ in range(NT)] for _ in range(NCH)]
        self.cv = self.alloc([128, NCV], F32)
        self.cv_t = T("cv")
        self.dc = self.alloc([128, 64], F32)
        self.dc_t = T("dc")
        self.ones = self.alloc([128, 128], BF16)
        self.onesA = self.alloc([128, 128], BF16)
        self.onesB = self.alloc([128, 128], BF16)
        self.bones = self.alloc([128, 128], BF16)
        self.ident = self.alloc([128, 128], BF16)
        self.ident_t = T("ident")
        self.const_t = T("consts")
        self.rs = [self.alloc([128, TT], F32) for _ in range(2)]
        self.rs_t = [T(), T()]
        self.NSQ = 2
        self.sq = [self.alloc([128, TT], BF16) for _ in range(self.NSQ)]
        self.sq_t = [T() for _ in range(self.NSQ)]
        self.sq_cnt = 0
        self.rs_cnt = 0
        self.WA = [self.alloc([128, 3072], BF16) for _ in range(3)]
        self.WA_t = [T() for _ in range(3)]
        self.WA_sem = [self.P.new_dma_sem() for _ in range(3)]
        self.NWB = 2 * GMAX
        self.WBflat = self.alloc([128, self.NWB * 1024], BF16)
        self.WBall = self.WBflat.rearrange("p (a b) -> p a b", a=self.NWB)
        self.WB_t = [T() for _ in range(self.NWB)]
        self.WB_sem = [self.P.new_dma_sem() for _ in range(self.NWB)]
        self.WA2_hw_sem = self.P.new_dma_sem()
        self.norm_done = set()
        self.arena_base = self.off
        self.sem_c = self.P.new_dma_sem()
        self.sem_x = [self.P.new_dma_sem() for _ in range(NCH)]
        self.sem_o = self.P.new_dma_sem()
        self.sem_misc = self.P.new_dma_sem()
        self.wa_cnt = 0
        self.out_ops = []

    def alloc(self, shape, dtype):
        esz = 4 if dtype == F32 else 2
        n = 1
        for s in shape[1:]:
            n *= s
        nb = n * esz
        off = self.off
        self.off = (off + nb + 63) // 64 * 64
        assert self.off <= self.arena_bytes, ("SBUF overflow", self.off, self.arena_bytes)
        v = self.arena_t[:, off // 2: off // 2 + nb // 2]
        if dtype == F32:
            v = v.bitcast(F32)
        if len(shape) == 3:
            v = v.rearrange("p (a b) -> p a b", a=shape[1])
        elif len(shape) == 4:
            v = v.rearrange("p (a b c) -> p a b c", a=shape[1], b=shape[2])
        return v

    def phase_begin(self):
        self.off = self.arena_base
        if getattr(self, "pre_norm_lasts", None):
            self.phase_lasts = self.pre_norm_lasts
            self.pre_norm_lasts = None
        else:
            self.phase_lasts = [self.P.ops[e][-1] for e in ("pe", "act", "dve") if self.P.ops[e]]
        for e in ("pe", "act", "dve"):
            self.P.pending[e].extend(self.phase_lasts)

    def mm(self, bank, out_ap, lhsT, rhs, start, stop, reads):
        self.P.op("pe", lambda e: e.matmul(out_ap, lhsT, rhs, start=start, stop=stop),
                  reads=reads, writes=[self.ps_t[bank]])

    def act(self, out, in_, func, reads, writes, scale=None, bias=None):
        kw = {}
        if scale is not None:
            kw["scale"] = scale
        if bias is not None:
            kw["bias"] = bias
        self.P.op("act", lambda e: e.activation(out=out, in_=in_, func=func, **kw), reads=reads, writes=writes)

    def stt(self, out, in0, scalar, in1, op0, op1, reads, writes):
        self.P.op("dve", lambda e: e.scalar_tensor_tensor(out=out, in0=in0, scalar=scalar, in1=in1, op0=op0, op1=op1),
                  reads=reads, writes=writes)

    def tt(self, out, in0, in1, op, reads, writes, eng="dve"):
        self.P.op(eng, lambda e: e.tensor_tensor(out=out, in0=in0, in1=in1, op=op), reads=reads, writes=writes)

    def ts(self, out, in0, s1, s2, op0, op1, reads, writes, eng="dve"):
        if op1 is None:
            self.P.op(eng, lambda e: e.tensor_scalar(out=out, in0=in0, scalar1=s1, scalar2=None, op0=op0),
                      reads=reads, writes=writes)
        else:
            self.P.op(eng, lambda e: e.tensor_scalar(out=out, in0=in0, scalar1=s1, scalar2=s2, op0=op0, op1=op1),
                      reads=reads, writes=writes)

    def cvcol(self, c, n=1):
        return self.cv[:, c:c + n]

    def dccol(self, c, n=1):
        return self.dc[:, c:c + n]

    def init(self):
        P = self.P
        P.dma("sp", lambda e: e.dma_start(out=self.cv, in_=self.cvec_d), self.sem_c, writes=[self.cv_t])
        for t in range(NT):
            grp = []
            for c in range(NCH):
                grp.append(P.dma("sp", lambda e, c=c, t=t: e.dma_start(out=self.xT[:, c, t * TT:(t + 1) * TT],
                                                                        in_=self.xT_d[c * 128:(c + 1) * 128, t * TT:(t + 1) * TT]),
                                 self.sem_x[t], writes=[self.x_t[c][t]]))
            for op in grp:
                op.dma_val = grp[-1].dma_val
        ct = [self.const_t]
        P.op("dve", lambda e: e.memset(self.ones, 1.0), writes=ct)
        P.op("dve", lambda e: e.memset(self.onesA, 0.0), writes=ct)
        P.op("dve", lambda e: e.memset(self.onesA[:, 0:64], 1.0), writes=ct)
        P.op("dve", lambda e: e.memset(self.onesB, 0.0), writes=ct)
        P.op("dve", lambda e: e.memset(self.onesB[:, 64:128], 1.0), writes=ct)
        P.op("dve", lambda e: e.memset(self.bones, 0.0), writes=ct)
        P.op("dve", lambda e: e.memset(self.bones[0:64, 0:64], 1.0), writes=ct)
        P.op("dve", lambda e: e.memset(self.bones[64:128, 64:128], 1.0), writes=ct)
        P.dma("pool", lambda e: e.dma_start(out=self.ident, in_=self.ident_d), self.sem_misc, writes=[self.ident_t])
        rd = [self.cv_t, self.dc_t]
        wr = [self.dc_t]
        lam = self.cvcol(CV_LAM, 8)
        s0 = self.dc[:, 40:48]
        s1 = self.dc[:, 48:56]
        s2 = self.dc[:, 56:64]
        cl = self.dc[:, 8:16]
        self.ts(self.dc[:, 0:1], self.cvcol(CV_KG), 8.0, None, ALU.mult, None, rd, wr)
        self.ts(s1, lam, -1.0, None, ALU.mult, None, rd, wr)
        self.tt(s0, s1, lam, ALU.max, rd, wr)
        self.act(s0, s0, AF.Exp, rd, wr, scale=-1.0)
        self.ts(s1, s0, 1.0, None, ALU.add, None, rd, wr)
        self.act(s2, s1, AF.Ln, rd, wr)
        self.ts(s1, s1, -1.0, 1e-30, ALU.add, ALU.max, rd, wr)
        P.op("dve", lambda e: e.reciprocal(out=s1, in_=s1), reads=rd, writes=wr)
        self.tt(s2, s2, s0, ALU.mult, rd, wr)
        self.tt(s2, s2, s1, ALU.mult, rd, wr)
        self.ts(s0, lam, -1.0, 0.0, ALU.mult, ALU.max, rd, wr)
        self.tt(s2, s2, s0, ALU.add, rd, wr)
        self.ts(cl, s2, -8.0, None, ALU.mult, None, rd, wr)
        self.ts(self.dc[:, 16:24], cl, 0.5, None, ALU.mult, None, rd, wr)
        self.ts(self.dc[:, 24:32], self.cvcol(CV_BA, 8), 0.5, None, ALU.mult, None, rd, wr)
        self.ts(self.dc[:, 32:40], self.cvcol(CV_BX, 8), 0.5, None, ALU.mult, None, rd, wr)

    def rmsnorm_tile(self, n, t):
        if (n, t) in self.norm_done:
            return
        self.norm_done.add((n, t))
        NB_ = 6
        tsl = slice(t * TT, (t + 1) * TT)
        for c in range(NCH):
            k = self.sq_cnt % self.NSQ
            self.sq_cnt += 1
            self.act(self.sq[k], self.xT[:, c, tsl], AF.Square, [self.x_t[c][t]], [self.sq_t[k]])
            self.mm(NB_, self.psum[NB_], self.ones, self.sq[k], c == 0, c == NCH - 1,
                    [self.sq_t[k], self.const_t])
        r = self.rs_cnt % 2
        self.rs_cnt += 1
        self.act(self.rs[r], self.psum[NB_], AF.Ln, [self.ps_t[NB_]], [self.rs_t[r]], scale=1.0 / D, bias=EPS)
        RB = 7
        self.act(self.psum[RB], self.rs[r], AF.Exp, [self.rs_t[r]], [self.ps_t[RB]], scale=-0.5)
        for c in range(NCH):
            self.stt(self.hT[:, c, tsl], self.xT[:, c, tsl], self.cvcol(CV_GAIN + n * 8 + c), self.psum[RB],
                     ALU.mult, ALU.mult, [self.x_t[c][t], self.ps_t[RB], self.cv_t], [self.h_t[c][t]])

    def rmsnorm(self, n):
        for t in range(NT):
            self.rmsnorm_tile(n, t)

    def norm_part(self, n, t, c):
        if c == 0:
            if (n, t) in self.norm_done:
                self._np_skip = True
                return
            self._np_skip = False
            self.norm_done.add((n, t))
        if self._np_skip:
            return
        tsl = slice(t * TT, (t + 1) * TT)
        k = self.sq_cnt % self.NSQ
        self.sq_cnt += 1
        self.act(self.sq[k], self.xT[:, c, tsl], AF.Square, [self.x_t[c][t]], [self.sq_t[k]])
        self.mm(6, self.psum[6], self.ones, self.sq[k], c == 0, c == NCH - 1, [self.sq_t[k], self.const_t])
        if c == NCH - 1:
            r = self.rs_cnt % 2
            self.rs_cnt += 1
            self.act(self.rs[r], self.psum[6], AF.Ln, [self.ps_t[6]], [self.rs_t[r]], scale=1.0 / D, bias=EPS)
            RB = 7
            self.act(self.psum[RB], self.rs[r], AF.Exp, [self.rs_t[r]], [self.ps_t[RB]], scale=-0.5)
            for cc_ in range(NCH):
                self.stt(self.hT[:, cc_, tsl], self.xT[:, cc_, tsl], self.cvcol(CV_GAIN + n * 8 + cc_), self.psum[RB],
                         ALU.mult, ALU.mult, [self.x_t[cc_][t], self.ps_t[RB], self.cv_t], [self.h_t[cc_][t]])

    def tail_norm(self, t):
        nn = self.next_norm
        if nn is None:
            return
        if t == NT - 1:
            self.pre_norm_lasts = [self.P.ops[e][-1] for e in ("pe", "act", "dve") if self.P.ops[e]]
        if t >= 1:
            self.rmsnorm_tile(nn, t - 1)
        if t == NT - 1:
            self.rmsnorm_tile(nn, t)

    def ffn(self, fi, n):
        P = self.P
        self.phase_begin()
        actb = [self.alloc([128, GMAX, SEQ], BF16) for _ in range(2)]
        act_t = [[[T() for _ in range(NT)] for _ in range(GMAX)] for _ in range(2)]
        sg = [self.alloc([128, TT], BF16) for _ in range(2)]
        sg_t = [T(), T()]
        groups = []
        j0 = 0
        for g in GROUPS:
            groups.append(list(range(j0, j0 + g)))
            j0 += g
        wa_slot = {}

        def load_win(j):
            s = self.wa_cnt % 3
            self.wa_cnt += 1
            wa_slot[j] = s
            P.dma("pool", lambda e: e.dma_start(out=self.WA[s][:, 0:2048], in_=self.fwin_d[fi, j], max_dma_last_dim=4096),
                  self.WA_sem[s], writes=[self.WA_t[s]])

        def load_wout(gi):
            for jl, j in enumerate(groups[gi]):
                s = (gi % 2) * GMAX + jl
                P.dma("pool", lambda e, s=s, j=j: e.dma_start(out=self.WBall[:, s, :], in_=self.fwout_d[fi, j * 128:(j + 1) * 128, :],
                                                                max_dma_last_dim=4096),
                      self.WB_sem[s], writes=[self.WB_t[s]])

        for j in range(3):
            load_win(j)
        first_phase = (fi == 0)
        if not first_phase:
            load_wout(0)
            load_wout(1)
        self.rmsnorm(n)
        cnt = [0, 0]

        def UG(gi, jl, j):
            s = wa_slot[j]
            w = self.WA[s][:, 0:2048].rearrange("p (k m) -> p k m", k=8)
            for t in range(NT):
                b = cnt[0] % 2
                cnt[0] += 1
                tsl = slice(t * TT, (t + 1) * TT)
                for kc in range(NCH):
                    self.mm(b, self.psum[b], w[:, kc, 0:128], self.hT[:, kc, tsl], kc == 0, kc == NCH - 1,
                            [self.WA_t[s], self.h_t[kc][t]])
                for kc in range(NCH):
                    self.mm(2 + b, self.psum[2 + b], w[:, kc, 128:256], self.hT[:, kc, tsl], kc == 0, kc == NCH - 1,
                            [self.WA_t[s], self.h_t[kc][t]])
                self.act(sg[b], self.psum[b], AF.Silu, [self.ps_t[b]], [sg_t[b]])
                self.tt(actb[gi % 2][:, jl, tsl], sg[b], self.psum[2 + b], ALU.mult,
                        [sg_t[b], self.ps_t[2 + b]], [act_t[gi % 2][jl][t]])
            if j + 3 < NF:
                load_win(j + 3)
            if first_phase and j < 2:
                load_wout(j)

        def OUT(gi, is_last):
            grp = groups[gi]
            for t in range(NT):
                tsl = slice(t * TT, (t + 1) * TT)
                for c in range(NCH):
                    b = 4 + cnt[1] % 2
                    cnt[1] += 1
                    for jl, j in enumerate(grp):
                        s = (gi % 2) * GMAX + jl
                        self.mm(b, self.psum[b], self.WBall[:, s, c * 128:(c + 1) * 128], actb[gi % 2][:, jl, tsl],
                                jl == 0, jl == len(grp) - 1, [self.WB_t[s], act_t[gi % 2][jl][t]])
                    self.stt(self.xT[:, c, tsl], self.psum[b], 0.5, self.xT[:, c, tsl], ALU.mult, ALU.add,
                             [self.ps_t[b], self.x_t[c][t]], [self.x_t[c][t]])
                    if is_last and self.final_phase:
                        self.store_tile(c, t)
                    if is_last and self.next_norm is not None and t >= 1:
                        self.norm_part(self.next_norm, t - 1, c)
                if is_last:
                    self.tail_norm(t)
            if gi + 2 < len(groups):
                load_wout(gi + 2)

        for gi, grp in enumerate(groups):
            for jl, j in enumerate(grp):
                UG(gi, jl, j)
                if jl == 0 and gi > 0:
                    OUT(gi - 1, False)
        OUT(len(groups) - 1, True)

    def attention(self, n):
        P = self.P
        self.phase_begin()
        attnT = self.alloc([128, NCH, SEQ], BF16)
        at_t = [[T() for _ in range(NT)] for _ in range(NCH)]
        qT = self.alloc([128, SEQ], BF16)
        q_t = [T() for _ in range(NT)]
        kTz = [self.alloc([128, SEQ], BF16) for _ in range(2)]
        k_t = [[T() for _ in range(NT)] for _ in range(2)]
        vz = [self.alloc([128, 16, 128], BF16) for _ in range(2)]
        v_t = [[T() for _ in range(16)] for _ in range(2)]
        expB = self.alloc([128, 2, 640], BF16)
        eb_t = T("expB")
        NE = 3
        E = [self.alloc([128, 2, TT], BF16) for _ in range(NE)]
        E_t = [T() for _ in range(NE)]
        Osb = [self.alloc([128, TT], F32) for _ in range(2)]
        Osb_t = [T(), T()]
        P.op("pool", lambda e: e.memset(kTz[0], 0.0), writes=k_t[0], extra=self.phase_lasts)
        P.op("pool", lambda e: e.memset(kTz[1], 0.0), writes=k_t[1])
        P.op("pool", lambda e: e.memset(vz[0], 0.0), writes=v_t[0])
        P.op("pool", lambda e: e.memset(vz[1], 0.0), writes=v_t[1])

        def load_pair(i):
            s = i % 2
            P.dma("pool", lambda e: e.dma_start(out=self.WA[s], in_=self.awin_d[i], max_dma_last_dim=4096),
                  self.WA_sem[s], writes=[self.WA_t[s]])

        def load_bias(i):
            P.dma("sp", lambda e: e.dma_start(out=self.WA[2][:, 0:2560].bitcast(F32), in_=self.bias_d[i]),
                  self.WA2_hw_sem, writes=[self.WA_t[2]])

        load_pair(0)
        load_bias(0)
        load_pair(1)
        self.rmsnorm(n)
        pc = [0]
        cc = [0, 0]
        qg = self.cvcol(CV_QG)
        kg8 = self.dccol(0)
        for i in range(8):
            s = i % 2
            w = self.WA[s].rearrange("p (k m) -> p k m", k=8)
            wt = self.WA_t[s]
            bsrc = self.WA[2][:, 0:2560].bitcast(F32).rearrange("p (h q) -> p h q", h=2)
            self.act(expB, bsrc, AF.Copy, [self.WA_t[2]], [eb_t])
            P.op("dve", lambda e: e.memset(expB[64:128, :, 0:64], -30000.0), reads=[], writes=[eb_t])
            P.op("dve", lambda e: e.memset(expB[0:64, :, 576:640], -30000.0), reads=[], writes=[eb_t])
            if i + 1 < 8:
                load_bias(i + 1)
            pitems = [(which, t) for which in range(2) for t in range(NT)]
            pbank = {}

            def pstage1(m):
                which, t = pitems[m]
                tsl = slice(t * TT, (t + 1) * TT)
                b = pc[0] % 3
                pc[0] += 1
                pbank[m] = b
                for kc in range(NCH):
                    self.mm(b, self.psum[b], w[:, kc, which * 128:(which + 1) * 128], self.hT[:, kc, tsl],
                            kc == 0, kc == NCH - 1, [wt, self.h_t[kc][t]])
                k = self.sq_cnt % self.NSQ
                self.sq_cnt += 1
                pbank[(m, "sq")] = k
                self.act(self.sq[k], self.psum[b], AF.Square, [self.ps_t[b]], [self.sq_t[k]])

            def pstage2(m):
                which, t = pitems[m]
                tsl = slice(t * TT, (t + 1) * TT)
                b = pbank[m]
                k = pbank[(m, "sq")]
                self.mm(3, self.psum[3], self.bones, self.sq[k], True, True, [self.sq_t[k], self.const_t])
                r = self.rs_cnt % 2
                self.rs_cnt += 1
                self.act(self.rs[r], self.psum[3], AF.Ln, [self.ps_t[3]], [self.rs_t[r]], scale=1.0, bias=64.0 * EPS)
                self.act(self.rs[r], self.rs[r], AF.Exp, [self.rs_t[r]], [self.rs_t[r]], scale=-0.5)
                if which == 0:
                    self.stt(qT[:, tsl], self.psum[b], qg, self.rs[r], ALU.mult, ALU.mult,
                             [self.ps_t[b], self.rs_t[r], self.cv_t], [q_t[t]])
                else:
                    for hh in range(2):
                        ps_ = slice(hh * 64, (hh + 1) * 64)
                        self.stt(kTz[hh][ps_, tsl], self.psum[b][ps_, :], kg8[ps_, :], self.rs[r][ps_, :], ALU.mult, ALU.mult,
                                 [self.ps_t[b], self.rs_t[r], self.dc_t], [k_t[hh][t]])

            for m in range(len(pitems) + 1):
                if m < len(pitems):
                    pstage1(m)
                if m >= 1:
                    pstage2(m - 1)
            for kt in range(16):
                t = kt // 4
                b = pc[0] % 3
                pc[0] += 1
                ksl = slice(kt * 128, (kt + 1) * 128)
                for kc in range(NCH):
                    self.mm(b, self.psum[b][:, 0:128], self.hT[:, kc, ksl], w[:, kc, 256:384], kc == 0, kc == NCH - 1,
                            [wt, self.h_t[kc][t]])
                for hh in range(2):
                    cs = slice(hh * 64, (hh + 1) * 64)
                    P.op("dve", lambda e, hh=hh, kt=kt, cs=cs, b=b: e.tensor_copy(out=vz[hh][:, kt, cs], in_=self.psum[b][:, cs]),
                         reads=[self.ps_t[b]], writes=[v_t[hh][kt]])
            if i + 2 < 8:
                load_pair(i + 2)
            seq = []
            for t in range(NT):
                kts = [kt for kt in (4 * t, 4 * t - 1, 4 * t + 1, 4 * t - 2, 4 * t + 2, 4 * t - 3, 4 * t + 3, 4 * t - 4)
                       if 0 <= kt < 16]
                for idx, kt in enumerate(kts):
                    seq.append((t, kt, idx == 0, idx == len(kts) - 1))
            st = {}

            def cstage1(k):
                t, kt, first, last = seq[k]
                a = max(TT * t, 128 * kt)
                bnd = min(TT * t + TT, 128 * kt + 640)
                nq = bnd - a
                sb = 2 * (cc[1] % 3)
                r = cc[1] % NE
                cc[1] += 1
                st[k] = (a, bnd, nq, r)
                for hh in range(2):
                    self.mm(sb + hh, self.psum[sb + hh][:, 0:nq], kTz[hh][:, kt * 128:(kt + 1) * 128], qT[:, a:bnd], True, False,
                            [k_t[hh][kt // 4], q_t[t]])
                    self.mm(sb + hh, self.psum[sb + hh][:, 0:nq], self.ident, expB[:, hh, a - 128 * kt: bnd - 128 * kt], False, True,
                            [self.ident_t, eb_t])
                sview = self.pall[:, sb * 512:(sb + 2) * 512].rearrange("p (h q) -> p h q", h=2)[:, :, 0:nq]
                self.act(E[r][:, :, 0:nq], sview, AF.Exp, [self.ps_t[sb], self.ps_t[sb + 1]], [E_t[r]])

            def cstage2(k):
                t, kt, first, last = seq[k]
                a, bnd, nq, r = st[k]
                ob = 6
                db = 7
                osl = slice(a - TT * t, bnd - TT * t)
                for hh in range(2):
                    self.mm(ob, self.psum[ob][:, osl], vz[hh][:, kt, :], E[r][:, hh, 0:nq], first and hh == 0, last and hh == 1,
                            [v_t[hh][kt], E_t[r]])
                    self.mm(db, self.psum[db][:, osl], self.onesA if hh == 0 else self.onesB,
                            E[r][:, hh, 0:nq], first and hh == 0, last and hh == 1, [self.const_t, E_t[r]])
                if last:
                    rr = self.rs_cnt % 2
                    self.rs_cnt += 1
                    oo = cc[0] % 2
                    cc[0] += 1
                    self.act(self.rs[rr], self.psum[db], AF.Ln, [self.ps_t[db]], [self.rs_t[rr]])
                    P.op("dve", lambda e: e.tensor_copy(out=Osb[oo], in_=self.psum[ob]), reads=[self.ps_t[ob]], writes=[Osb_t[oo]])
                    self.act(self.rs[rr], self.rs[rr], AF.Exp, [self.rs_t[rr]], [self.rs_t[rr]], scale=-1.0)
                    self.tt(attnT[:, i, t * TT:(t + 1) * TT], Osb[oo], self.rs[rr], ALU.mult,
                            [Osb_t[oo], self.rs_t[rr]], [at_t[i][t]])

            LA = 2
            for k in range(len(seq) + LA):
                if k < len(seq):
                    cstage1(k)
                if k >= LA:
                    cstage2(k - LA)
        self.out_proj(self.awout_d, attnT, at_t)

    def out_proj(self, w_d, yT, y_t, extra=()):
        P = self.P
        for i in range(8):
            P.dma("pool", lambda e, i=i: e.dma_start(out=self.WBall[:, i, :], in_=w_d[i * 128:(i + 1) * 128, :], max_dma_last_dim=4096),
                  self.WB_sem[i], writes=[self.WB_t[i]], extra=extra)
        cnt = 0
        for t in range(NT):
            tsl = slice(t * TT, (t + 1) * TT)
            for c in range(NCH):
                b = cnt % 2
                cnt += 1
                for i in range(8):
                    self.mm(b, self.psum[b], self.WBall[:, i, c * 128:(c + 1) * 128], yT[:, i, tsl], i == 0, i == 7,
                            [self.WB_t[i], y_t[i][t]])
                self.tt(self.xT[:, c, tsl], self.psum[b], self.xT[:, c, tsl], ALU.add,
                        [self.ps_t[b], self.x_t[c][t]], [self.x_t[c][t]])
                if self.next_norm is not None and t >= 1:
                    self.norm_part(self.next_norm, t - 1, c)
            self.tail_norm(t)

    def lru(self, n):
        P = self.P
        self.phase_begin()
        H = 2 * TT
        yT = self.alloc([128, NCH, SEQ], BF16)
        y_t = [[T() for _ in range(NT)] for _ in range(NCH)]
        bd = [self.alloc([128, 2, 128], F32) for _ in range(2)]
        bd_t = [T(), T()]
        bd_sem = [P.new_dma_sem(), P.new_dma_sem()]
        halo = [self.alloc([128, 4], F32) for _ in range(2)]
        halo_t = [T(), T()]
        carry = self.alloc([128, 4], F32)
        carry_t = T()
        names = ("xc", "tha", "thx", "Ab", "gw")
        bufs = [{}, {}]
        tts = [{}, {}]
        for j, nm in enumerate(names):
            bufs[0][nm] = self.alloc([128, H], F32)
            bufs[1][nm] = self.WBflat[:, j * 2 * H:(j + 1) * 2 * H].bitcast(F32)
            tts[0][nm] = T()
            tts[1][nm] = T()
        xc3 = [bufs[0]["xc"], bufs[1]["xc"], self.alloc([128, H], F32)]
        xc3_t = [tts[0]["xc"], tts[1]["xc"], T()]
        lbd = self.lbd_d.rearrange("p (c s m) -> p c s m", c=8, s=2)
        xb2 = self.pall[:, 0:2 * TT]
        g2 = self.pall[:, 2 * TT:4 * TT]
        ra2 = self.pall[:, 4 * TT:6 * TT]
        rx2 = self.pall[:, 6 * TT:8 * TT]
        XB, GB, RA, RX = [self.ps_t[0], self.ps_t[1]], [self.ps_t[2], self.ps_t[3]], [self.ps_t[4], self.ps_t[5]], [self.ps_t[6], self.ps_t[7]]

        def load_bd(c, extra=()):
            b = c % 2
            P.dma("sp", lambda e: e.dma_start(out=bd[b], in_=lbd[:, c, :, :]), bd_sem[b], writes=[bd_t[b]], extra=extra)

        def load_w(c):
            s = self.wa_cnt % 3
            self.wa_cnt += 1
            P.dma("pool", lambda e: e.dma_start(out=self.WA[s][:, 0:2048], in_=self.lwin_d[c], max_dma_last_dim=4096),
                  self.WA_sem[s], writes=[self.WA_t[s]])
            return s

        slots = {}
        for c in range(3):
            slots[c] = load_w(c)
        load_bd(0, extra=self.phase_lasts)
        load_bd(1, extra=self.phase_lasts)
        self.rmsnorm(n)
        units = [(c, m) for c in range(NCH) for m in range(2)]
        nu = len(units)
        xbv = [self.pall[:, 0:2 * TT], self.pall[:, 2 * TT:4 * TT]]
        XBT = [[self.ps_t[0], self.ps_t[1]], [self.ps_t[2], self.ps_t[3]]]
        g2 = self.pall[:, 4 * TT:6 * TT]
        GB = [self.ps_t[4], self.ps_t[5]]

        def XB(k):
            c, m = units[k]
            s = slots[c]
            w = self.WA[s][:, 0:2048].rearrange("p (k m) -> p k m", k=8)
            wt = self.WA_t[s]
            pb = 2 * (k % 2)
            for hf in range(2):
                t = 2 * m + hf
                tsl = slice(t * TT, (t + 1) * TT)
                for kc in range(NCH):
                    self.mm(pb + hf, self.psum[pb + hf], w[:, kc, 0:128], self.hT[:, kc, tsl], kc == 0, kc == NCH - 1,
                            [wt, self.h_t[kc][t]])

        def CONV(k):
            c, m = units[k]
            u = k % 2
            pu = (k - 1) % 2
            xb2 = xbv[k % 2]
            XBk = XBT[k % 2]
            cw = [self.cvcol(CV_CW + j * 8 + c) for j in range(4)]
            cb = self.cvcol(CV_CB + c)
            P.op("dve", lambda e: e.tensor_copy(out=halo[u][:, 0:3], in_=xb2[:, H - 3:H]), reads=XBk, writes=[halo_t[u]])
            xc = xc3[k % 3]
            xct = xc3_t[k % 3]
            self.ts(xc, xb2, cw[3], cb, ALU.mult, ALU.add, XBk + [self.cv_t], [xct])
            for j in (2, 1, 0):
                sh = 3 - j
                self.stt(xc[:, sh:H], xb2[:, 0:H - sh], cw[j], xc[:, sh:H], ALU.mult, ALU.add,
                         XBk + [self.cv_t, xct], [xct])
                if m > 0:
                    self.stt(xc[:, 0:sh], halo[pu][:, 3 - sh:3], cw[j], xc[:, 0:sh], ALU.mult, ALU.add,
                             [halo_t[pu], self.cv_t, xct], [xct])

        def G(k):
            c, m = units[k]
            u = k % 2
            B_, Tt = bufs[u], tts[u]
            xc = xc3[k % 3]
            xct = xc3_t[k % 3]
            b = c % 2
            hba = self.dccol(24 + c)
            hbx = self.dccol(32 + c)
            for hf in range(2):
                hs = slice(hf * TT, (hf + 1) * TT)
                self.mm(6, self.psum[6], bd[b][:, 0, :], xc[:, hs], True, True, [bd_t[b], xct])
                self.mm(7, self.psum[7], bd[b][:, 1, :], xc[:, hs], True, True, [bd_t[b], xct])
                self.act(B_["tha"][:, hs], self.psum[6], AF.Tanh, [self.ps_t[6], self.dc_t], [Tt["tha"]], scale=0.5, bias=hba)
                self.act(B_["thx"][:, hs], self.psum[7], AF.Tanh, [self.ps_t[7], self.dc_t], [Tt["thx"]], scale=0.5, bias=hbx)
            if m == 1 and c + 2 < NCH:
                load_bd(c + 2)

        def A2(k):
            c, m = units[k]
            s = slots[c]
            w = self.WA[s][:, 0:2048].rearrange("p (k m) -> p k m", k=8)
            wt = self.WA_t[s]
            for hf in range(2):
                t = 2 * m + hf
                tsl = slice(t * TT, (t + 1) * TT)
                for kc in range(NCH):
                    self.mm(4 + hf, self.psum[4 + hf], w[:, kc, 128:256], self.hT[:, kc, tsl], kc == 0, kc == NCH - 1,
                            [wt, self.h_t[kc][t]])
        def B_rest(k):
            c, m = units[k]
            u = k % 2
            B_, Tt = bufs[u], tts[u]
            cl = self.dccol(8 + c)
            hcl = self.dccol(16 + c)
            tha, Ab, gw = B_["tha"], B_["Ab"], B_["gw"]
            self.act(Ab, tha, AF.Exp, [Tt["tha"], self.dc_t], [Tt["Ab"]], scale=hcl, bias=hcl)
            self.act(tha, tha, AF.Exp, [Tt["tha"], self.dc_t], [Tt["tha"]], scale=cl, bias=cl)
            self.ts(tha, tha, 1.0, None, ALU.min, None, [Tt["tha"]], [Tt["tha"]])
            self.act(gw, g2, AF.Square, GB, [Tt["gw"]], scale=0.21145921)
            self.stt(gw, gw, 1.0, g2, ALU.add, ALU.mult, [Tt["gw"]] + GB, [Tt["gw"]])
            self.act(gw, gw, AF.Tanh, [Tt["gw"]], [Tt["gw"]], scale=0.7978845608)
            self.stt(gw, gw, 1.0, g2, ALU.add, ALU.mult, [Tt["gw"]] + GB, [Tt["gw"]])
            self.act(tha, tha, AF.Sqrt, [Tt["tha"]], [Tt["tha"]], scale=-0.0625, bias=0.0625)

        def C(k):
            c, m = units[k]
            u = k % 2
            B_, Tt = bufs[u], tts[u]
            tha, thx, Ab, gw = B_["tha"], B_["thx"], B_["Ab"], B_["gw"]
            xc = xc3[k % 3]
            xct = xc3_t[k % 3]
            self.ts(thx, thx, 1.0, 1.0, ALU.add, ALU.mult, [Tt["thx"]], [Tt["thx"]], eng="pool")
            self.tt(thx, thx, xc, ALU.mult, [Tt["thx"], xct], [Tt["thx"]], eng="pool")
            self.tt(thx, thx, tha, ALU.mult, [Tt["thx"], Tt["tha"]], [Tt["thx"]], eng="pool")
            if m == 0:
                P.op("dve", lambda e: e.tensor_tensor_scan(out=tha, data0=Ab, data1=thx, initial=0.0,
                                                           op0=ALU.mult, op1=ALU.add),
                     reads=[Tt["Ab"], Tt["thx"]], writes=[Tt["tha"]])
                P.op("dve", lambda e: e.tensor_copy(out=carry[:, 0:1], in_=tha[:, H - 1:H]), reads=[Tt["tha"]], writes=[carry_t])
            else:
                P.op("dve", lambda e: e.tensor_tensor_scan(out=tha, data0=Ab, data1=thx,
                                                           initial=carry[:, 0:1], op0=ALU.mult, op1=ALU.add),
                     reads=[Tt["Ab"], Tt["thx"], carry_t], writes=[Tt["tha"]])
            self.tt(yT[:, c, m * H:(m + 1) * H], gw, tha, ALU.mult, [Tt["gw"], Tt["tha"]],
                    [y_t[c][2 * m], y_t[c][2 * m + 1]], eng="pool")

        XB(0)
        CONV(0)
        G(0)
        A2(0)
        XB(1)
        CONV(1)
        XB(2)
        for k in range(nu):
            B_rest(k)
            if k + 1 < nu:
                G(k + 1)
                A2(k + 1)
            if k + 2 < nu:
                CONV(k + 2)
            if k + 3 < nu:
                XB(k + 3)
            C(k)
            if k + 1 < nu:
                c1, m1 = units[k + 1]
                if m1 == 1 and c1 + 3 < NCH:
                    slots[c1 + 3] = load_w(c1 + 3)
        lasts = [P.ops[e][-1] for e in ("pe", "act", "dve", "pool")]
        self.out_proj(self.lwout_d, yT, y_t, extra=lasts)

    def store_tile(self, c, t):
        tsl = slice(t * TT, (t + 1) * TT)
        op = self.P.dma("sp", lambda e: e.dma_start(out=self.out_d[c * 128:(c + 1) * 128, tsl], in_=self.xT[:, c, tsl]),
                        self.sem_o, reads=[self.x_t[c][t]])
        self.out_ops.append(op)
        self.stored.add((c, t))

    def finish(self):
        P = self.P
        for t in range(NT):
            for c in range(NCH):
                if (c, t) not in self.stored:
                    self.store_tile(c, t)
        for op in self.out_ops:
            op.dma_val = self.out_ops[-1].dma_val
        P.emit(self.nc, final_wait_ops=self.out_ops)

    def build(self):
        self.init()
        subs = [lambda: self.ffn(0, 0), lambda: self.attention(1), lambda: self.ffn(1, 2),
                lambda: self.ffn(2, 3), lambda: self.lru(4), lambda: self.ffn(3, 5)]
        self.stored = set()
        for k in range(self.n_sub):
            self.next_norm = (k + 1) if k + 1 < self.n_sub else None
            self.final_phase = (k == self.n_sub - 1)
            subs[k]()
        self.finish()
        return self.nc


def prep_shared(inp):
    f32 = np.float32
    cvec = np.zeros((128, NCV), f32)
    norms = [inp["norm_ffn_pre"][0], inp["norm_mix"][0], inp["norm_ffn_post"][0],
             inp["norm_ffn_pre"][1], inp["norm_mix"][1], inp["norm_ffn_post"][1]]
    for n, g in enumerate(norms):
        cvec[:, CV_GAIN + n * 8: CV_GAIN + n * 8 + 8] = np.asarray(g, f32).reshape(8, 128).T
    cvec[:, CV_QG] = np.tile(np.asarray(inp["attn_q_gain"][0], f32), 2)
    cvec[:, CV_KG] = np.tile(np.asarray(inp["attn_k_gain"][0], f32), 2)
    cw = np.asarray(inp["lru_conv_w"][0], f32)
    for j in range(4):
        cvec[:, CV_CW + j * 8: CV_CW + j * 8 + 8] = cw[j].reshape(8, 128).T
    cvec[:, CV_CB:CV_CB + 8] = np.asarray(inp["lru_conv_b"][0], f32).reshape(8, 128).T
    cvec[:, CV_BA:CV_BA + 8] = np.asarray(inp["lru_b_a"][0], f32).reshape(8, 128).T
    cvec[:, CV_BX:CV_BX + 8] = np.asarray(inp["lru_b_x"][0], f32).reshape(8, 128).T
    cvec[:, CV_LAM:CV_LAM + 8] = np.asarray(inp["lru_lambda"][0], f32).reshape(8, 128).T

    def win_tiles(w):
        w = np.asarray(w, f32).reshape(8, 128, 2, NF, 128)
        return np.ascontiguousarray(w.transpose(3, 1, 0, 2, 4)).reshape(NF, 128, 2048)

    ffn_win = np.stack([win_tiles(inp["ffn_pre_w_in"][0]), win_tiles(inp["ffn_post_w_in"][0]),
                        win_tiles(inp["ffn_pre_w_in"][1]), win_tiles(inp["ffn_post_w_in"][1])])
    ffn_wout = np.stack([np.asarray(inp["ffn_pre_w_out"][0], f32), np.asarray(inp["ffn_post_w_out"][0], f32),
                         np.asarray(inp["ffn_pre_w_out"][1], f32), np.asarray(inp["ffn_post_w_out"][1], f32)])
    aw = np.asarray(inp["attn_w_in"][0], f32).reshape(8, 128, 3, 8, 128)
    attn_win = np.ascontiguousarray(aw.transpose(3, 1, 0, 2, 4)).reshape(8, 128, 3072)
    rb = np.asarray(inp["attn_rel_bias"][0], f32)
    ko = np.arange(128)[:, None]
    qo = np.arange(640)[None, :]
    rel = np.clip(qo - ko, -128, 128) + 128
    bt = rb[:, rel]
    bias_tab = np.ascontiguousarray(bt.reshape(8, 2, 128, 640).transpose(0, 2, 1, 3)).reshape(8, 128, 1280)
    lw = np.asarray(inp["lru_w_in"][0], f32).reshape(8, 128, 2, 8, 128)
    lru_win = np.ascontiguousarray(lw.transpose(3, 1, 0, 2, 4)).reshape(8, 128, 2048)
    bd = np.zeros((128, 8, 2, 128), f32)
    for si, key in enumerate(("lru_w_a", "lru_w_x")):
        wb = np.asarray(inp[key][0], f32)
        for c in range(8):
            bd[0:64, c, si, 0:64] = wb[2 * c]
            bd[64:128, c, si, 64:128] = wb[2 * c + 1]
    return {
        "cvec": cvec, "ffn_win": ffn_win, "ffn_wout": ffn_wout, "attn_win": attn_win,
        "attn_wout": np.ascontiguousarray(np.asarray(inp["attn_w_out"][0], f32)),
        "bias_tab": bias_tab, "lru_win": lru_win, "lru_bd": bd.reshape(128, 2048),
        "lru_wout": np.ascontiguousarray(np.asarray(inp["lru_w_out"][0], f32)),
        "ident": np.eye(128, dtype=f32),
    }


_NC_CACHE = {}


def get_nc(n_sub=6):
    if n_sub not in _NC_CACHE:
        _NC_CACHE[n_sub] = Builder(n_sub).build()
    return _NC_CACHE[n_sub]


def kernel(**inputs):
    x = np.asarray(inputs["x"], np.float32)
    shared = prep_shared(inputs)
    in_maps = []
    for b in range(NB):
        m = dict(shared)
        m["xT"] = np.ascontiguousarray(x[b].T)
        in_maps.append(m)
    nc = Builder(6).build()
    res = run_bass_kernel_spmd(nc, in_maps, core_ids=list(range(NB)))
    out = np.stack([np.ascontiguousarray(res.results[b]["outT"].T) for b in range(NB)])
    return out.astype(np.float32)
```
